# Optimizing a Trainium2 kernel written in Bass

```python
import jax, jax.numpy as jnp
from jax import lax
import numpy as np

D_MODEL = 2048
BATCH = 4
SEQ = 8192
DEPTH = 4

HEAD_DIM = 128
N_HEADS_A = 12
N_HEADS_B = 12
WIDTH_A = N_HEADS_A * HEAD_DIM
WIDTH_B = N_HEADS_B * HEAD_DIM
DILATED_PATTERNS = ((128, 1), (512, 4), (2048, 16))
Q_BLOCK = 128
ROPE_THETA = 10000.0
LN_EPS = 1e-5
DEEPNORM_ALPHA = float((2 * DEPTH) ** 0.25)
DEEPNORM_BETA = float((8 * DEPTH) ** -0.25)
FORGET_BIAS_INIT = 2.0

SPLIT_SIZES = (WIDTH_A, WIDTH_A, WIDTH_A, WIDTH_A,
               WIDTH_B, WIDTH_B, WIDTH_B, WIDTH_B,
               N_HEADS_B,
               2 * D_MODEL)
SPLIT_POINTS = tuple(int(v) for v in np.cumsum(SPLIT_SIZES)[:-1])
N_IN_COLS = int(sum(SPLIT_SIZES))

kernel_name = "hybrid_dilated_forgetting_attention_deepnorm"


def rope(t, pos):
    half = HEAD_DIM // 2
    inv_freq = ROPE_THETA ** (-jnp.arange(half, dtype=jnp.float32) / half)
    ang = pos.astype(jnp.float32)[:, None] * inv_freq[None, :]
    cos = jnp.cos(ang)[None, :, None, :]
    sin = jnp.sin(ang)[None, :, None, :]
    t32 = t.astype(jnp.float32)
    t1, t2 = t32[..., :half], t32[..., half:]
    return jnp.concatenate([t1 * cos - t2 * sin, t1 * sin + t2 * cos], axis=-1).astype(t.dtype)


def dilated_pattern(q, k, v, window, dilation):
    B, S, H, hd = q.shape
    span = window // dilation
    L = S // dilation
    nblk = -(-L // Q_BLOCK)
    Lp = nblk * Q_BLOCK

    def to_residue(t):
        t = t.reshape(B, L, dilation, H, hd).transpose(0, 2, 1, 3, 4)
        return jnp.pad(t, ((0, 0), (0, 0), (0, Lp - L), (0, 0), (0, 0)))

    def band_keys(t):
        tp = jnp.pad(t, ((0, 0), (0, 0), (Q_BLOCK, 0), (0, 0), (0, 0)))
        tp = tp.reshape(B, dilation, nblk + 1, Q_BLOCK, H, hd)
        return jnp.concatenate([tp[:, :, :-1], tp[:, :, 1:]], axis=3)

    qb = to_residue(q).reshape(B, dilation, nblk, Q_BLOCK, H, hd)
    kb = band_keys(to_residue(k))
    vb = band_keys(to_residue(v))

    n_idx = jnp.arange(nblk)[:, None, None]
    i_idx = jnp.arange(Q_BLOCK)[None, :, None]
    j_idx = jnp.arange(2 * Q_BLOCK)[None, None, :]
    dist = Q_BLOCK + i_idx - j_idx
    valid = (dist >= 0) & (dist <= span) & (n_idx * Q_BLOCK + j_idx - Q_BLOCK >= 0)
    valid = valid[None, None, :, None]

    scale = HEAD_DIM ** -0.5
    s = jnp.einsum('bdnihe,bdnjhe->bdnhij', qb, kb, preferred_element_type=jnp.float32) * scale
    s = jnp.where(valid, s, -jnp.inf)
    m = jnp.max(s, axis=-1, keepdims=True)
    p = jnp.exp(s - m)
    den = jnp.sum(p, axis=-1, keepdims=True)
    out = jnp.einsum('bdnhij,bdnjhe->bdnihe', p, vb.astype(jnp.float32))
    out = out / jnp.moveaxis(den, 3, 4)
    lse = jnp.moveaxis((m + jnp.log(den))[..., 0], 3, 4)

    out = out.reshape(B, dilation, Lp, H, hd)[:, :, :L].transpose(0, 2, 1, 3, 4).reshape(B, S, H, hd)
    lse = lse.reshape(B, dilation, Lp, H)[:, :, :L].transpose(0, 2, 1, 3).reshape(B, S, H)
    return out, lse


def dilated_mixture(q, k, v):
    outs, lses = [], []
    for window, dilation in DILATED_PATTERNS:
        o, l = dilated_pattern(q, k, v, window, dilation)
        outs.append(o)
        lses.append(l)
    w = jax.nn.softmax(jnp.stack(lses, axis=0), axis=0)
    return jnp.sum(w[..., None] * jnp.stack(outs, axis=0), axis=0).astype(q.dtype)


def forgetting_attention(q, k, v, log_f):
    B, S, H, hd = q.shape
    nblk = S // Q_BLOCK
    cum = jnp.cumsum(log_f, axis=1).transpose(0, 2, 1)
    q_blocks = q.reshape(B, nblk, Q_BLOCK, H, hd).transpose(1, 0, 2, 3, 4)
    c_blocks = cum.reshape(B, H, nblk, Q_BLOCK).transpose(2, 0, 1, 3)
    kpos = jnp.arange(S)
    scale = HEAD_DIM ** -0.5

    def block(args):
        n, qn, cn = args
        s = jnp.einsum('bihe,bjhe->bhij', qn, k, preferred_element_type=jnp.float32) * scale
        s = s + cn[..., :, None] - cum[..., None, :]
        qpos = n * Q_BLOCK + jnp.arange(Q_BLOCK)
        causal = kpos[None, :] <= qpos[:, None]
        s = jnp.where(causal[None, None], s, -jnp.inf)
        p = jax.nn.softmax(s, axis=-1)
        return jnp.einsum('bhij,bjhe->bihe', p, v.astype(jnp.float32)).astype(q.dtype)

    out = lax.map(block, (jnp.arange(nblk), q_blocks, c_blocks))
    return out.transpose(1, 0, 2, 3, 4).reshape(B, S, H, hd)


def layer_norm(x, g, b):
    x32 = x.astype(jnp.float32)
    mu = jnp.mean(x32, axis=-1, keepdims=True)
    var = jnp.mean(jnp.square(x32 - mu), axis=-1, keepdims=True)
    y = (x32 - mu) * lax.rsqrt(var + LN_EPS)
    return (y * g.astype(jnp.float32) + b.astype(jnp.float32)).astype(x.dtype)


def hybrid_layer(x, pos, w_in, b_forget, b_gate, w_up_a, w_up_b, w_out, ln_g, ln_b):
    B, S, _ = x.shape
    h = jnp.einsum('bsd,dc->bsc', x, w_in)
    qa, ka, va, za, qb, kb, vb, zb, f_logit, g_logit = jnp.split(h, SPLIT_POINTS, axis=-1)
    heads = lambda t, n: t.reshape(B, S, n, HEAD_DIM)

    qa = rope(heads(qa, N_HEADS_A), pos)
    ka = rope(heads(ka, N_HEADS_A), pos)
    out_a = dilated_mixture(qa, ka, heads(va, N_HEADS_A)).reshape(B, S, WIDTH_A)
    up_a = jnp.einsum('bsw,wd->bsd', out_a * jax.nn.silu(za), w_up_a)

    log_f = jax.nn.log_sigmoid((f_logit + b_forget).astype(jnp.float32))
    out_b = forgetting_attention(heads(qb, N_HEADS_B), heads(kb, N_HEADS_B),
                                 heads(vb, N_HEADS_B), log_f).reshape(B, S, WIDTH_B)
    up_b = jnp.einsum('bsw,wd->bsd', out_b * jax.nn.silu(zb), w_up_b)

    gates = jax.nn.sigmoid(g_logit + b_gate)
    g_a, g_b = gates[..., :D_MODEL], gates[..., D_MODEL:]
    y = jnp.einsum('bsd,de->bse', g_a * up_a + g_b * up_b, w_out)

    return layer_norm(DEEPNORM_ALPHA * x + y, ln_g, ln_b)


def setup_inputs(seed: int = 0) -> dict:
    key = jax.random.key(seed)
    ks = jax.random.split(key, 10)
    x = jax.random.normal(ks[0], (BATCH, SEQ, D_MODEL), jnp.float32)

    offs = (0,) + SPLIT_POINTS
    col_scale = jnp.ones((N_IN_COLS,), jnp.float32)
    col_scale = col_scale.at[offs[2]:offs[3]].set(DEEPNORM_BETA)
    col_scale = col_scale.at[offs[6]:offs[7]].set(DEEPNORM_BETA)
    w_in = jax.random.normal(ks[1], (DEPTH, D_MODEL, N_IN_COLS), jnp.float32) * (D_MODEL ** -0.5) * col_scale
    b_forget = FORGET_BIAS_INIT + 0.1 * jax.random.normal(ks[2], (DEPTH, N_HEADS_B), jnp.float32)
    b_gate = 0.02 * jax.random.normal(ks[3], (DEPTH, 2 * D_MODEL), jnp.float32)
    w_up_a = jax.random.normal(ks[4], (DEPTH, WIDTH_A, D_MODEL), jnp.float32) * (WIDTH_A ** -0.5) * DEEPNORM_BETA
    w_up_b = jax.random.normal(ks[5], (DEPTH, WIDTH_B, D_MODEL), jnp.float32) * (WIDTH_B ** -0.5) * DEEPNORM_BETA
    w_out = jax.random.normal(ks[6], (DEPTH, D_MODEL, D_MODEL), jnp.float32) * (D_MODEL ** -0.5) * DEEPNORM_BETA
    ln_g = 1.0 + 0.02 * jax.random.normal(ks[7], (DEPTH, D_MODEL), jnp.float32)
    ln_b = 0.02 * jax.random.normal(ks[8], (DEPTH, D_MODEL), jnp.float32)
    return {"x": x, "w_in": w_in, "b_forget": b_forget, "b_gate": b_gate,
            "w_up_a": w_up_a, "w_up_b": w_up_b, "w_out": w_out,
            "ln_g": ln_g, "ln_b": ln_b}


def reference(x, w_in, b_forget, b_gate, w_up_a, w_up_b, w_out, ln_g, ln_b):
    pos = jnp.arange(x.shape[1], dtype=jnp.int32)
    for l in range(DEPTH):
        x = hybrid_layer(x, pos, w_in[l], b_forget[l], b_gate[l],
                         w_up_a[l], w_up_b[l], w_out[l], ln_g[l], ln_b[l])
    return x
```

```python
import contextlib
import numpy as np
import ml_dtypes
import concourse.bass as bass
import concourse.mybir as mybir
from concourse.bass_utils import run_bass_kernel_spmd

F32 = mybir.dt.float32
BF16 = mybir.dt.bfloat16
AF = mybir.ActivationFunctionType
ALU = mybir.AluOpType

D = 2048
S = 8192
DEPTH = 4
HD = 128
NHA = 12
NH = 6
W = NH * HD
WB = 768
NCH = D // 128
NEG = -30000.0
ALPHA = float((2 * DEPTH) ** 0.25)
LN_EPS = 1e-5
SCALE = HD ** -0.5
PATTERNS = ((128, 1), (512, 4), (2048, 16))
NCOL = 8 * W + 4096 + NH
C_QA, C_KA, C_ZA, C_QB, C_KB, C_ZB, C_G, C_VA, C_VB = 0, W, 2 * W, 3 * W, 4 * W, 5 * W, 6 * W, 6 * W + 4096, 7 * W + 4096


class TK:
    def __init__(self, nc):
        self.nc = nc
        self.eng = {'pe': nc.tensor, 'act': nc.scalar, 'dve': nc.vector, 'pool': nc.gpsimd, 'sp': nc.sync}
        self.semh = {}
        self.cnt = {}
        for k in ('pe', 'act', 'dve', 'pool'):
            self.semh[k] = nc.alloc_semaphore('s_' + k)
            self.cnt[k] = 0
        self.semh['cc'] = nc.alloc_semaphore('s_cc')
        self.ncc = 0
        self.pending = {k: False for k in self.cnt}
        self.waited = {k: {} for k in self.eng}
        self.lastw = {}
        self.readers = {}
        self.dq = {}
        for q in ('sp', 'pool', 'act'):
            sems = []
            for i in range(8):
                name = 'd_%s%d' % (q, i)
                self.semh[name] = nc.alloc_semaphore(name)
                sems.append(name)
            self.dq[q] = dict(eng=q, sems=sems, i=0, ev=[None] * 8)

    def _wait(self, e, ev):
        if ev is None:
            return
        sk, val = ev
        if e == 'pe' and sk == 'pe':
            return
        w = self.waited[e]
        if w.get(sk, 0) >= val:
            return
        self.eng[e].wait_ge(self.semh[sk], val)
        w[sk] = val

    def _deps(self, e, reads, writes):
        for k in reads:
            self._wait(e, self.lastw.get(k))
        for k in writes:
            self._wait(e, self.lastw.get(k))
            r = self.readers.get(k)
            if r:
                for sk, val in r.items():
                    self._wait(e, (sk, val))

    def _commit(self, ev, reads, writes):
        for k in writes:
            self.lastw[k] = ev
            self.readers[k] = {}
        for k in reads:
            r = self.readers.setdefault(k, {})
            if r.get(ev[0], 0) < ev[1]:
                r[ev[0]] = ev[1]

    def op(self, e, fn, reads=(), writes=(), sig=True):
        self._deps(e, reads, writes)
        ins = fn()
        if sig:
            self.cnt[e] += 1
            ins.then_inc(self.semh[e], 1)
            self.pending[e] = False
            ev = (e, self.cnt[e])
        else:
            ev = (e, self.cnt[e] + 1)
            self.pending[e] = True
        self._commit(ev, reads, writes)
        return ev

    def dma(self, q, out, in_, reads=(), writes=()):
        Q = self.dq[q]
        i = Q['i']
        slot = i % 8
        e = Q['eng']
        self._wait(e, Q['ev'][slot])
        self._deps(e, reads, writes)
        sk = Q['sems'][slot]
        self.eng[e].dma_start(out=out, in_=in_).then_inc(self.semh[sk], 16)
        ev = (sk, 16 * (i // 8 + 1))
        Q['ev'][slot] = ev
        Q['i'] = i + 1
        self._commit(ev, reads, writes)
        return ev

    def cc(self, fn, reads=(), writes=()):
        self._deps('pool', reads, writes)
        fn().then_inc(self.semh['cc'])
        self.ncc += 1
        ev = ('cc', self.ncc)
        self._commit(ev, reads, writes)
        return ev

    def barrier(self, engines=('pe', 'act', 'dve', 'pool', 'sp')):
        assert not any(self.pending.values()), self.pending
        evs = [(k, self.cnt[k]) for k in self.cnt if self.cnt[k] > 0]
        if self.ncc:
            evs.append(('cc', self.ncc))
        for Q in self.dq.values():
            evs += [ev for ev in Q['ev'] if ev is not None]
        for e in engines:
            for ev in evs:
                if e == ev[0]:
                    continue
                self._wait(e, ev)
        self.lastw = {}
        self.readers = {}


class Builder:
    def __init__(self, n_layers=DEPTH, debug=(), collective=True):
        self.n_layers = n_layers
        self.debug = debug
        self.collective = collective
        nc = self.nc = bass.Bass("TRN2", target_bir_lowering=False)
        self.tk = TK(nc)
        dt0 = nc.dram_tensor

        def dt(name, shape, dtype, kind):
            if kind is None:
                kind = "ExternalOutput" if name in debug else "Internal"
            return dt0(name, shape, dtype, kind=kind)
        kind_s = None
        self.x = dt("x", [S, D], F32, kind="ExternalInput").ap()
        self.w_sel = dt("w_sel", [n_layers, D, NCOL], F32, kind="ExternalInput").ap()
        self.w_upa = dt("w_upa", [n_layers, W, D], F32, kind="ExternalInput").ap()
        self.w_upb = dt("w_upb", [n_layers, W, D], F32, kind="ExternalInput").ap()
        self.w_out = dt("w_out", [n_layers, D, D], F32, kind="ExternalInput").ap()
        self.bfr = dt("bfr", [n_layers, 128, NH], F32, kind="ExternalInput").ap()
        self.bgt = dt("bgt", [n_layers, 128, 32], F32, kind="ExternalInput").ap()
        self.lng = dt("lng", [n_layers, 128, NCH], F32, kind="ExternalInput").ap()
        self.lnb = dt("lnb", [n_layers, 128, NCH], F32, kind="ExternalInput").ap()
        self.cosT = dt("cosT", [128, S], F32, kind="ExternalInput").ap()
        self.sinT = dt("sinT", [128, S], F32, kind="ExternalInput").ap()
        self.cf32 = dt("cf32", [128, 4 * 128], F32, kind="ExternalInput").ap()
        self.cbf = dt("cbf", [128, 2 * 128 + 4 * 512 + 256], BF16, kind="ExternalInput").ap()
        self.out = dt("out", [S, D], F32, kind="ExternalOutput").ap()
        self.xTf = dt("xTf", [D, S], F32, kind=kind_s).ap()
        self.xTb = dt("xTb", [D, S], BF16, kind=kind_s).ap()
        self.QAT = dt("QAT", [W, S], BF16, kind=kind_s).ap()
        self.KAT = dt("KAT", [W, S], BF16, kind=kind_s).ap()
        self.ZAT = dt("ZAT", [W, S], BF16, kind=kind_s).ap()
        self.QBT = dt("QBT", [W, S], BF16, kind=kind_s).ap()
        self.KBT = dt("KBT", [W, S], BF16, kind=kind_s).ap()
        self.ZBT = dt("ZBT", [W, S], BF16, kind=kind_s).ap()
        self.VA = dt("VA", [S, W], BF16, kind=kind_s).ap()
        self.VB = dt("VB", [S, W], BF16, kind=kind_s).ap()
        self.GT = dt("GT", [4096, S], BF16, kind=kind_s).ap()
        self.GAT = dt("GAT", [W, S], BF16, kind=kind_s).ap()
        self.GBT = dt("GBT", [W, S], BF16, kind=kind_s).ap()
        self.yT = dt("yT", [16, D, 512], F32, kind=kind_s).ap()
        self.yTs = dt("yTs", [16, D, 512], F32, kind=kind_s).ap()
        self.mT = dt("mT", [D, S], BF16, kind=kind_s).ap()
        sb = nc.alloc_sbuf_tensor
        self.c_f32 = sb("c_f32", [128, 512], F32)
        self.c_bf = sb("c_bf", [128, 2 * 128 + 4 * 512 + 256], BF16)
        self.LU = sb("LU", [128, 64, NH], F32)
        self.cumL = sb("cumL", [128, 64, NH], F32)
        self.carry = sb("carry", [128, 65, NH], F32)
        self.ps = [nc.alloc_psum_tensor("ps%d" % i, [128, 512], F32) for i in range(8)]
        self.ident = self.c_f32[:, 0:128]
        self.ones = self.c_f32[:, 128:256]
        self.tri = self.c_f32[:, 256:384]
        self.perm = self.c_f32[:, 384:512]
        self.ident_b = self.c_bf[:, 0:128]
        self.ones_b = self.c_bf[:, 128:256]
        self.fmask = [self.c_bf[:, 256 + j * 512: 256 + (j + 1) * 512] for j in range(4)]
        self.dmask = self.c_bf[:, 256 + 2048: 256 + 2048 + 256]

    def _sbt(self, es):
        self.uid = getattr(self, 'uid', 0) + 1
        u = self.uid
        return lambda n, s, d: es.enter_context(self.nc.sbuf_tensor("%s_%d" % (n, u), s, d))

    def pe(self, fn, reads=(), writes=(), sig=True):
        return self.tk.op('pe', fn, reads, writes, sig)

    def act(self, fn, reads=(), writes=()):
        return self.tk.op('act', fn, reads, writes)

    def dve(self, fn, reads=(), writes=()):
        return self.tk.op('dve', fn, reads, writes)

    def pool(self, fn, reads=(), writes=()):
        return self.tk.op('pool', fn, reads, writes)

    def load_consts(self):
        self.tk.dma('sp', self.c_f32[:], self.cf32[:, :], writes=['c_f32'])
        self.tk.dma('sp', self.c_bf[:], self.cbf[:, :], writes=['c_bf'])

    def phase_x0(self):
        nc, tk = self.nc, self.tk
        xTf_v = self.xTf.rearrange("(c p) t -> p c t", p=128)
        xTb_v = self.xTb.rearrange("(c p) t -> p c t", p=128)
        with contextlib.ExitStack() as es:
            sbt = self._sbt(es)
            xin = [sbt("x0_in%d" % i, [128, D], F32) for i in range(2)]
            stf = [sbt("x0_sf%d" % i, [128, NCH, 512], F32) for i in range(2)]
            stb = [sbt("x0_sb%d" % i, [128, NCH, 512], BF16) for i in range(2)]
            g = 0
            for tt in range(16):
                for sub in range(4):
                    i = tt * 4 + sub
                    xi = xin[i % 2]
                    tk.dma('sp', xi[:], self.x[i * 128:(i + 1) * 128, :], writes=[('xin', i % 2)])
                    for cg in range(4):
                        bk = g % 4
                        g += 1
                        bank = self.ps[bk]
                        for c4 in range(4):
                            c = cg * 4 + c4
                            self.pe(lambda: nc.tensor.transpose(out=bank[:, c4 * 128:(c4 + 1) * 128],
                                                                in_=xi[:, c * 128:(c + 1) * 128], identity=self.ident),
                                    reads=[('xin', i % 2), 'c_f32'], writes=[('ps', bk)], sig=(c4 == 3))
                        src = bank[:, :].rearrange("p (c t) -> p c t", c=4)
                        self.act(lambda: nc.scalar.copy(out=stf[tt % 2][:, cg * 4:cg * 4 + 4, sub * 128:(sub + 1) * 128], in_=src),
                                 reads=[('ps', bk)], writes=[('stf', tt % 2)])
                        self.pool(lambda: nc.gpsimd.tensor_copy(stb[tt % 2][:, cg * 4:cg * 4 + 4, sub * 128:(sub + 1) * 128],
                                                                stf[tt % 2][:, cg * 4:cg * 4 + 4, sub * 128:(sub + 1) * 128]),
                                  reads=[('stf', tt % 2)], writes=[('stb', tt % 2)])
                tk.dma('pool', xTf_v[:, :, tt * 512:(tt + 1) * 512], stf[tt % 2][:], reads=[('stf', tt % 2)])
                tk.dma('pool', xTb_v[:, :, tt * 512:(tt + 1) * 512], stb[tt % 2][:], reads=[('stb', tt % 2)])
            tk.barrier()

    def phase_p(self, l):
        nc, tk = self.nc, self.tk
        xTb_v = self.xTb.rearrange("(c p) t -> p c t", p=128)
        blocks = []
        for (kind, dstT, c0) in (('rope', self.QAT, C_QA), ('rope', self.KAT, C_KA), ('silu', self.ZAT, C_ZA),
                                 ('copy', self.QBT, C_QB), ('copy', self.KBT, C_KB), ('silu', self.ZBT, C_ZB)):
            for hb in range(W // WB):
                blocks.append((kind, dstT[hb * WB:(hb + 1) * WB, :], c0 + hb * WB, WB))
        for gi in range(4):
            blocks.append(('gate', self.GT[gi * 1024:(gi + 1) * 1024, :], C_G + gi * 1024, 1024))
        nvb = W // WB
        for hb in range(nvb):
            blocks.append(('v', self.VA[:, hb * WB:(hb + 1) * WB], C_VA + hb * WB, WB))
        for hb in range(nvb):
            last = hb == nvb - 1
            blocks.append(('vf' if last else 'v', self.VB[:, hb * WB:(hb + 1) * WB], C_VB + hb * WB, WB + (NH if last else 0)))
        with contextlib.ExitStack() as es:
            sbt = self._sbt(es)
            wst = [sbt("p_wst%d" % i, [128, 1024], F32) for i in range(4)]
            wbf = [sbt("p_wbf%d" % i, [128, NCH, 1024], BF16) for i in range(2)]
            xt = [sbt("p_xt%d" % i, [128, NCH, 512], BF16) for i in range(3)]
            cs = [sbt("p_cos%d" % i, [128, 512], F32) for i in range(2)]
            sn = [sbt("p_sin%d" % i, [128, 512], F32) for i in range(2)]
            stg = [sbt("p_stg%d" % i, [128, 8, 512], BF16) for i in range(2)]
            stv = [sbt("p_stv%d" % i, [128, 4, WB], BF16) for i in range(2)]
            qf = [sbt("p_qf%d" % i, [128, 512], F32) for i in range(3)]
            t1 = [sbt("p_t1%d" % i, [128, 512], F32) for i in range(2)]
            t2 = [sbt("p_t2%d" % i, [128, 512], F32) for i in range(2)]
            bg = sbt("p_bg", [128, 32], F32)
            bf = sbt("p_bf", [128, NH], F32)
            tk.dma('sp', bg[:], self.bgt[l], writes=['bg'])
            tk.dma('sp', bf[:], self.bfr[l], writes=['bf'])
            wcount = [0]
            xcount = [0]
            gcount = [0]
            rcount = [0]
            scount = [0]

            def load_w(bi):
                kind, dst, c0, ncols = blocks[bi]
                par = bi % 2
                for dch in range(NCH):
                    k = wcount[0] % 4
                    wcount[0] += 1
                    tk.dma('sp', wst[k][:, 0:ncols], self.w_sel[l, dch * 128:(dch + 1) * 128, c0:c0 + ncols],
                           writes=[('wst', k)])
                    self.pool(lambda: nc.gpsimd.tensor_copy(wbf[par][:, dch, 0:ncols], wst[k][:, 0:ncols]),
                              reads=[('wst', k)], writes=[('wbf', par)])

            def load_x(tt, rope):
                k = xcount[0] % 3
                xcount[0] += 1
                tk.dma('sp', xt[k][:], xTb_v[:, :, tt * 512:(tt + 1) * 512], writes=[('xt', k)])
                if rope:
                    tk.dma('sp', cs[tt % 2][:], self.cosT[:, tt * 512:(tt + 1) * 512], writes=[('cs', tt % 2)])
                    tk.dma('sp', sn[tt % 2][:], self.sinT[:, tt * 512:(tt + 1) * 512], writes=[('sn', tt % 2)])
                return k

            load_w(0)
            for bi, (kind, dst, c0, ncols) in enumerate(blocks):
                par = bi % 2
                rope = kind == 'rope'
                xk_next = load_x(0, rope)
                if bi + 1 < len(blocks):
                    load_w(bi + 1)
                for tt in range(16):
                    xk = xk_next
                    if tt + 1 < 16:
                        xk_next = load_x(tt + 1, rope)
                    X = xt[xk]
                    if kind in ('v', 'vf'):
                        sv = scount[0] % 2
                        scount[0] += 1
                        for sub in range(4):
                            for (cc0, cn) in ((0, 512), (512, ncols - 512)):
                                bk = gcount[0] % 4
                                gcount[0] += 1
                                bank = self.ps[bk]
                                for dch in range(NCH):
                                    self.pe(lambda: nc.tensor.matmul(bank[:, 0:cn], lhsT=X[:, dch, sub * 128:(sub + 1) * 128],
                                                                     rhs=wbf[par][:, dch, cc0:cc0 + cn],
                                                                     start=(dch == 0), stop=(dch == NCH - 1)),
                                            reads=[('xt', xk), ('wbf', par)], writes=[('ps', bk)], sig=(dch == NCH - 1))
                                nv = min(cn, WB - cc0)
                                if kind == 'vf' and cc0 == 512:
                                    self.dve(lambda: nc.vector.tensor_copy(stv[sv][:, sub, cc0:cc0 + nv], bank[:, 0:nv]),
                                             reads=[('ps', bk)], writes=[('stv', sv)])
                                else:
                                    self.act(lambda: nc.scalar.copy(out=stv[sv][:, sub, cc0:cc0 + nv], in_=bank[:, 0:nv]),
                                             reads=[('ps', bk)], writes=[('stv', sv)])
                                if kind == 'vf' and cc0 == 512:
                                    self.dve(lambda: nc.vector.tensor_tensor(out=self.LU[:, tt * 4 + sub, :], in0=bank[:, nv:nv + NH],
                                                                             in1=bf[:], op=ALU.add),
                                             reads=[('ps', bk), 'bf'], writes=['LU'])
                        dv = dst[tt * 512:(tt + 1) * 512, :].rearrange("(n p) c -> p n c", p=128)
                        tk.dma('pool', dv, stv[sv][:], reads=[('stv', sv)])
                        continue
                    nchunk = ncols // 128
                    sg = scount[0] % 2
                    scount[0] += 1
                    pend = []
                    for oc in range(nchunk):
                        bk = gcount[0] % 4
                        gcount[0] += 1
                        bank = self.ps[bk]
                        for dch in range(NCH):
                            self.pe(lambda: nc.tensor.matmul(bank[:, :], lhsT=wbf[par][:, dch, oc * 128:(oc + 1) * 128],
                                                             rhs=X[:, dch, :], start=(dch == 0), stop=(dch == NCH - 1)),
                                    reads=[('xt', xk), ('wbf', par)], writes=[('ps', bk)], sig=(dch == NCH - 1))
                        if kind == 'copy':
                            self.act(lambda: nc.scalar.copy(out=stg[sg][:, oc, :], in_=bank[:, :]),
                                     reads=[('ps', bk)], writes=[('stg', sg)])
                        elif kind == 'silu':
                            self.act(lambda: nc.scalar.activation(out=stg[sg][:, oc, :], in_=bank[:, :], func=AF.Silu),
                                     reads=[('ps', bk)], writes=[('stg', sg)])
                        elif kind == 'gate':
                            gc = (c0 - C_G) // 128 + oc
                            self.act(lambda: nc.scalar.activation(out=stg[sg][:, oc, :], in_=bank[:, :], func=AF.Sigmoid,
                                                                  bias=bg[:, gc:gc + 1], scale=1.0),
                                     reads=[('ps', bk), 'bg'], writes=[('stg', sg)])
                        else:
                            qk = rcount[0] % 3
                            rcount[0] += 1
                            self.act(lambda: nc.scalar.copy(out=qf[qk][:], in_=bank[:, :]),
                                     reads=[('ps', bk)], writes=[('qf', qk)])
                            pend.append((oc, qk))
                            if len(pend) > 1:
                                self._rope_finish(pend.pop(0), tt, sg, stg, qf, t1, t2, cs, sn)
                    while pend:
                        self._rope_finish(pend.pop(0), tt, sg, stg, qf, t1, t2, cs, sn)
                    dv = dst.rearrange("(c p) t -> p c t", p=128)[:, :, tt * 512:(tt + 1) * 512]
                    tk.dma('pool', dv, stg[sg][:, 0:nchunk, :], reads=[('stg', sg)])
            LUf = self.LU[:].rearrange("p k h -> p (k h)")
            self.act(lambda: nc.scalar.activation(out=LUf, in_=LUf, func=AF.Exp, scale=-1.0), reads=['LU'], writes=['LU'])
            self.act(lambda: nc.scalar.activation(out=LUf, in_=LUf, func=AF.Ln, bias=1.0, scale=1.0), reads=['LU'], writes=['LU'])
            n = 32 * NH
            tot = sbt("p_tot", [128, 64, NH], F32)
            totf = tot[:].rearrange("p k h -> p (k h)")
            for hf in range(2):
                self.pe(lambda: nc.tensor.matmul(self.ps[hf][:, 0:n], lhsT=self.tri, rhs=LUf[:, hf * n:(hf + 1) * n], start=True, stop=True),
                        reads=['LU', 'c_f32'], writes=[('ps', hf)])
                self.pe(lambda: nc.tensor.matmul(self.ps[2 + hf][:, 0:n], lhsT=self.ones, rhs=LUf[:, hf * n:(hf + 1) * n], start=True, stop=True),
                        reads=['LU', 'c_f32'], writes=[('ps', 2 + hf)])
                self.dve(lambda: nc.vector.tensor_copy(totf[:, hf * n:(hf + 1) * n], self.ps[2 + hf][:, 0:n]),
                         reads=[('ps', 2 + hf)], writes=['tot'])
            self.dve(lambda: nc.vector.memset(self.carry[:, 0, :], 0.0), writes=['carry'])
            for kt in range(64):
                self.dve(lambda: nc.vector.tensor_tensor(out=self.carry[:, kt + 1, :], in0=self.carry[:, kt, :], in1=tot[:, kt, :], op=ALU.add),
                         reads=['carry', 'tot'], writes=['carry'])
            cumf = self.cumL[:].rearrange("p k h -> p (k h)")
            carf = self.carry[:, 0:64, :].rearrange("p k h -> p (k h)")
            for hf in range(2):
                self.dve(lambda: nc.vector.tensor_tensor(out=cumf[:, hf * n:(hf + 1) * n], in0=self.ps[hf][:, 0:n],
                                                         in1=carf[:, hf * n:(hf + 1) * n], op=ALU.add),
                         reads=[('ps', hf), 'carry'], writes=['cumL'])
            tk.barrier()

    def _rope_finish(self, item, tt, sg, stg, qf, t1, t2, cs, sn):
        nc = self.nc
        oc, qk = item
        bk = 4 + (oc % 2)
        bank = self.ps[bk]
        k2 = oc % 2
        self.pe(lambda: nc.tensor.matmul(bank[:, :], lhsT=self.perm, rhs=qf[qk][:], start=True, stop=True),
                reads=[('qf', qk), 'c_f32'], writes=[('ps', bk)])
        self.pool(lambda: nc.gpsimd.tensor_tensor(out=t1[k2][:], in0=qf[qk][:], in1=cs[tt % 2][:], op=ALU.mult),
                  reads=[('qf', qk), ('cs', tt % 2)], writes=[('t1', k2)])
        self.dve(lambda: nc.vector.tensor_tensor(out=t2[k2][:], in0=bank[:, :], in1=sn[tt % 2][:], op=ALU.mult),
                 reads=[('ps', bk), ('sn', tt % 2)], writes=[('t2', k2)])
        self.dve(lambda: nc.vector.tensor_tensor(out=stg[sg][:, oc, :], in0=t1[k2][:], in1=t2[k2][:], op=ALU.add),
                 reads=[('t1', k2), ('t2', k2)], writes=[('stg', sg)])

    def phase_a(self):
        nc, tk = self.nc, self.tk
        with contextlib.ExitStack() as es:
            sbt = self._sbt(es)
            QT = sbt("a_q", [128, S], BF16)
            KT = sbt("a_k", [128, S], BF16)
            V = [sbt("a_v%d" % i, [128, 64, HD], BF16) for i in range(3)]
            accn = sbt("a_accn", [128, S], F32)
            accd = sbt("a_accd", [128, S], F32)
            PT = [sbt("a_pt%d" % i, [128, 256], BF16) for i in range(3)]
            ZT = [sbt("a_z%d" % i, [128, 2048], BF16) for i in range(2)]
            og = [sbt("a_og%d" % i, [128, 2048], BF16) for i in range(2)]
            it = 0
            zc = 0
            for h in range(NH):
                hs = slice(h * 128, (h + 1) * 128)
                tk.dma('sp', QT[:], self.QAT[hs, :], writes=['aq'])
                tk.dma('sp', KT[:], self.KAT[hs, :], writes=['ak'])
                for pi, (win, d) in enumerate(PATTERNS):
                    nm = 64 // d
                    src = self.VA[:, hs].rearrange("(m p r) e -> r p m e", p=128, r=d)
                    for r in range(d):
                        tk.dma('sp', V[pi][:, r * nm:(r + 1) * nm, :], src[r], writes=[('av', pi)])
                for pi, (win, d) in enumerate(PATTERNS):
                    nm = 64 // d
                    for r in range(d):
                        for n in range(nm):
                            sb_, nb, db = it % 2, 2 + it % 2, 4 + it % 2
                            pk = it % 3
                            it += 1
                            sbank, nbank, dbank = self.ps[sb_], self.ps[nb], self.ps[db]
                            qsl = slice(n * 128 * d + r, n * 128 * d + r + 127 * d + 1, d)
                            c0 = 0 if n > 0 else 128
                            self.pe(lambda: nc.tensor.matmul(sbank[:, c0:256], lhsT=self.ident_b, rhs=self.dmask[:, c0:256],
                                                             start=True, stop=False),
                                    reads=['c_bf'], writes=[('ps', sb_)], sig=False)
                            for j in ((0, 1) if n > 0 else (1,)):
                                m = n - 1 + j
                                ksl = slice(m * 128 * d + r, m * 128 * d + r + 127 * d + 1, d)
                                self.pe(lambda: nc.tensor.matmul(sbank[:, j * 128:(j + 1) * 128], lhsT=KT[:, ksl], rhs=QT[:, qsl],
                                                                 start=False, stop=(j == 1)),
                                        reads=['aq', 'ak'], writes=[('ps', sb_)], sig=(j == 1))
                            self.act(lambda: nc.scalar.activation(out=PT[pk][:, c0:256], in_=sbank[:, c0:256], func=AF.Exp, scale=SCALE),
                                     reads=[('ps', sb_)], writes=[('apt', pk)])
                            for j in ((0, 1) if n > 0 else (1,)):
                                m = n - 1 + j
                                first = (j == 0) or n == 0
                                self.pe(lambda: nc.tensor.matmul(nbank[:, 0:128], lhsT=V[pi][:, r * nm + m, :], rhs=PT[pk][:, j * 128:(j + 1) * 128],
                                                                 start=first, stop=(j == 1)),
                                        reads=[('av', pi), ('apt', pk)], writes=[('ps', nb)], sig=(j == 1))
                            for j in ((0, 1) if n > 0 else (1,)):
                                first = (j == 0) or n == 0
                                self.pe(lambda: nc.tensor.matmul(dbank[:, 0:128], lhsT=self.ones_b, rhs=PT[pk][:, j * 128:(j + 1) * 128],
                                                                 start=first, stop=(j == 1)),
                                        reads=['c_bf', ('apt', pk)], writes=[('ps', db)], sig=(j == 1))
                            if pi == 0:
                                self.dve(lambda: nc.vector.tensor_copy(accn[:, qsl], nbank[:, 0:128]),
                                         reads=[('ps', nb)], writes=['accn'])
                                self.dve(lambda: nc.vector.tensor_copy(accd[:, qsl], dbank[:, 0:128]),
                                         reads=[('ps', db)], writes=['accd'])
                            else:
                                self.dve(lambda: nc.vector.tensor_tensor(out=accn[:, qsl], in0=nbank[:, 0:128], in1=accn[:, qsl], op=ALU.add),
                                         reads=[('ps', nb), 'accn'], writes=['accn'])
                                self.dve(lambda: nc.vector.tensor_tensor(out=accd[:, qsl], in0=dbank[:, 0:128], in1=accd[:, qsl], op=ALU.add),
                                         reads=[('ps', db), 'accd'], writes=['accd'])
                for c in range(4):
                    zk = zc % 2
                    zc += 1
                    csl = slice(c * 2048, (c + 1) * 2048)
                    tk.dma('sp', ZT[zk][:], self.ZAT[hs, csl], writes=[('az', zk)])
                    self.dve(lambda: nc.vector.reciprocal(accd[:, csl], accd[:, csl]), reads=['accd'], writes=['accd'])
                    self.pool(lambda: nc.gpsimd.tensor_tensor(out=accn[:, csl], in0=accn[:, csl], in1=accd[:, csl], op=ALU.mult),
                              reads=['accn', 'accd'], writes=['accn'])
                    self.pool(lambda: nc.gpsimd.tensor_tensor(out=og[zk][:], in0=accn[:, csl], in1=ZT[zk][:], op=ALU.mult),
                              reads=['accn', ('az', zk)], writes=[('aog', zk)])
                    tk.dma('pool', self.GAT[hs, csl], og[zk][:], reads=[('aog', zk)])
            tk.barrier()

    def phase_b(self):
        nc, tk = self.nc, self.tk
        with contextlib.ExitStack() as es:
            sbt = self._sbt(es)
            QT = [sbt("b_q%d" % i, [128, S], BF16) for i in range(2)]
            KT = [sbt("b_k%d" % i, [128, S], BF16) for i in range(2)]
            V = [sbt("b_v%d" % i, [128, 64, HD], BF16) for i in range(2)]
            bias = [sbt("b_bias%d" % i, [128, 64, 32], F32) for i in range(2)]
            PT = [sbt("b_pt%d" % i, [128, 512], BF16) for i in range(4)]
            ZT = [sbt("b_z%d" % i, [128, 512], BF16) for i in range(2)]
            rd = [sbt("b_rd%d" % i, [128, 512], F32) for i in range(2)]
            on = [sbt("b_on%d" % i, [128, 512], F32) for i in range(2)]
            og = [sbt("b_og%d" % i, [128, 512], BF16) for i in range(2)]

            def load_head(h):
                hp = h % 2
                hs = slice(h * 128, (h + 1) * 128)
                tk.dma('sp', QT[hp][:], self.QBT[hs, :], writes=[('bq', hp)])
                tk.dma('sp', KT[hp][:], self.KBT[hs, :], writes=[('bk', hp)])
                tk.dma('sp', V[hp][:], self.VB[:, hs].rearrange("(m p) e -> p m e", p=128), writes=[('bv', hp)])
                for qs in range(32):
                    self.dve(lambda: nc.vector.tensor_scalar(out=bias[hp][:, :, qs], in0=self.cumL[:, :, h],
                                                             scalar1=self.carry[:, 2 * qs + 1, h:h + 1], scalar2=None, op0=ALU.subtract),
                             reads=['cumL', 'carry'], writes=[('bbias', hp)])

            load_head(0)
            it = 0
            qi = 0
            for h in range(NH):
                hp = h % 2
                hs = slice(h * 128, (h + 1) * 128)
                if h + 1 < NH:
                    load_head(h + 1)
                for qt in range(16):
                    nk = 4 * qt + 4
                    q0 = qt * 512
                    nb, db = 4 + qi % 2, 6 + qi % 2
                    zk = qi % 2
                    qi += 1
                    nbank, dbank = self.ps[nb], self.ps[db]
                    tk.dma('sp', ZT[zk][:], self.ZBT[hs, q0:q0 + 512], writes=[('bz', zk)])

                    def emit_s(kt):
                        sb_ = (it + kt) % 3
                        sbank = self.ps[sb_]
                        j = kt - 4 * qt
                        if j >= 0:
                            self.pe(lambda: nc.tensor.matmul(sbank[:, :], lhsT=self.ident_b, rhs=self.fmask[j], start=True, stop=False),
                                    reads=['c_bf'], writes=[('ps', sb_)], sig=False)
                        self.pe(lambda: nc.tensor.matmul(sbank[:, :], lhsT=KT[hp][:, kt * 128:(kt + 1) * 128], rhs=QT[hp][:, q0:q0 + 512],
                                                         start=(j < 0), stop=True),
                                reads=[('bq', hp), ('bk', hp)], writes=[('ps', sb_)])
                        pk = (it + kt) % 4
                        for hf in range(2):
                            qs = 2 * qt + hf
                            self.act(lambda: nc.scalar.activation(out=PT[pk][:, hf * 256:(hf + 1) * 256], in_=sbank[:, hf * 256:(hf + 1) * 256],
                                                                  func=AF.Exp, scale=SCALE, bias=bias[hp][:, kt, qs:qs + 1]),
                                     reads=[('ps', sb_), ('bbias', hp)], writes=[('bpt', pk)])

                    def emit_pv(kt):
                        pk = (it + kt) % 4
                        self.pe(lambda: nc.tensor.matmul(nbank[:, :], lhsT=V[hp][:, kt, :], rhs=PT[pk][:], start=(kt == 0), stop=(kt == nk - 1)),
                                reads=[('bv', hp), ('bpt', pk)], writes=[('ps', nb)], sig=(kt == nk - 1))
                        self.pe(lambda: nc.tensor.matmul(dbank[:, :], lhsT=self.ones_b, rhs=PT[pk][:], start=(kt == 0), stop=(kt == nk - 1)),
                                reads=['c_bf', ('bpt', pk)], writes=[('ps', db)], sig=True)

                    LA = 2
                    for kt in range(min(LA, nk)):
                        emit_s(kt)
                    for kt in range(nk):
                        if kt + LA < nk:
                            emit_s(kt + LA)
                        emit_pv(kt)
                    it += nk
                    self.dve(lambda: nc.vector.reciprocal(rd[zk][:], dbank[:, :]), reads=[('ps', db)], writes=[('brd', zk)])
                    self.dve(lambda: nc.vector.tensor_tensor(out=on[zk][:], in0=nbank[:, :], in1=rd[zk][:], op=ALU.mult),
                             reads=[('ps', nb), ('brd', zk)], writes=[('bon', zk)])
                    self.pool(lambda: nc.gpsimd.tensor_tensor(out=og[zk][:], in0=on[zk][:], in1=ZT[zk][:], op=ALU.mult),
                              reads=[('bon', zk), ('bz', zk)], writes=[('bog', zk)])
                    tk.dma('pool', self.GBT[hs, q0:q0 + 512], og[zk][:], reads=[('bog', zk)])
            tk.barrier()

    def phase_u(self, l):
        nc, tk = self.nc, self.tk
        GT_v = self.GT.rearrange("(c p) t -> p c t", p=128)
        GAT_v = self.GAT.rearrange("(c p) t -> p c t", p=128)
        GBT_v = self.GBT.rearrange("(c p) t -> p c t", p=128)
        mT_v = self.mT.rearrange("(c p) t -> p c t", p=128)
        TU = 256
        with contextlib.ExitStack() as es:
            sbt = self._sbt(es)
            wst = [sbt("u_wst%d" % i, [128, 1024], F32) for i in range(2)]
            wa = sbt("u_wa", [128, NH, D], BF16)
            wb = sbt("u_wb", [128, NH, D], BF16)
            ga = [sbt("u_ga%d" % i, [128, NH, TU], BF16) for i in range(2)]
            gb = [sbt("u_gb%d" % i, [128, NH, TU], BF16) for i in range(2)]
            gg = [sbt("u_gg%d" % i, [128, 2, TU], BF16) for i in range(4)]
            t1 = [sbt("u_t1%d" % i, [128, TU], F32) for i in range(2)]
            t2 = [sbt("u_t2%d" % i, [128, TU], F32) for i in range(2)]
            ms = [sbt("u_ms%d" % i, [128, NCH, TU], BF16) for i in range(2)]
            k = 0
            for (dstw, srcw, nrow) in ((wa, self.w_upa, NH), (wb, self.w_upb, NH)):
                for c in range(2 * nrow):
                    kk = k % 2
                    k += 1
                    hs_ = slice((c % 2) * 1024, (c % 2 + 1) * 1024)
                    tk.dma('sp', wst[kk][:], srcw[l, (c // 2) * 128:(c // 2 + 1) * 128, hs_], writes=[('uwst', kk)])
                    self.pool(lambda: nc.gpsimd.tensor_copy(dstw[:, c // 2, hs_], wst[kk][:]), reads=[('uwst', kk)], writes=['uw'])

            def load_t(tt):
                tp = tt % 2
                ts_ = slice(tt * TU, (tt + 1) * TU)
                tk.dma('sp', ga[tp][:], GAT_v[:, :, ts_], writes=[('uga', tp)])
                tk.dma('sp', gb[tp][:], GBT_v[:, :, ts_], writes=[('ugb', tp)])

            ntt = S // TU
            load_t(0)
            gi = 0
            bi = 0
            for tt in range(ntt):
                tp = tt % 2
                ts_ = slice(tt * TU, (tt + 1) * TU)
                if tt + 1 < ntt:
                    load_t(tt + 1)
                for dc in range(NCH):
                    gk = gi % 4
                    gi += 1
                    tk.dma('sp', gg[gk][:, 0, :], GT_v[:, dc, ts_], writes=[('ugg', gk)])
                    tk.dma('sp', gg[gk][:, 1, :], GT_v[:, NCH + dc, ts_], writes=[('ugg', gk)])
                    ba, bb = bi % 8, (bi + 1) % 8
                    bi += 2
                    for (bank_i, wt, gsrc, key) in ((ba, wa, ga, 'uga'), (bb, wb, gb, 'ugb')):
                        for wc in range(NH):
                            self.pe(lambda: nc.tensor.matmul(self.ps[bank_i][:, 0:TU], lhsT=wt[:, wc, dc * 128:(dc + 1) * 128], rhs=gsrc[tp][:, wc, :],
                                                             start=(wc == 0), stop=(wc == NH - 1)),
                                    reads=['uw', (key, tp)], writes=[('ps', bank_i)], sig=(wc == NH - 1))
                    k2 = dc % 2
                    self.dve(lambda: nc.vector.tensor_tensor(out=t1[k2][:], in0=self.ps[ba][:, 0:TU], in1=gg[gk][:, 0, :], op=ALU.mult),
                             reads=[('ps', ba), ('ugg', gk)], writes=[('ut1', k2)])
                    self.dve(lambda: nc.vector.tensor_tensor(out=t2[k2][:], in0=self.ps[bb][:, 0:TU], in1=gg[gk][:, 1, :], op=ALU.mult),
                             reads=[('ps', bb), ('ugg', gk)], writes=[('ut2', k2)])
                    self.pool(lambda: nc.gpsimd.tensor_tensor(out=ms[tp][:, dc, :], in0=t1[k2][:], in1=t2[k2][:], op=ALU.add),
                              reads=[('ut1', k2), ('ut2', k2)], writes=[('ums', tp)])
                tk.dma('pool', mT_v[:, :, ts_], ms[tp][:], reads=[('ums', tp)])
            tk.barrier()
        with contextlib.ExitStack() as es:
            sbt = self._sbt(es)
            wst = [sbt("u2_wst%d" % i, [128, 1024], F32) for i in range(2)]
            wo = sbt("u2_wo", [128, NCH, D], BF16)
            mt = [sbt("u2_m%d" % i, [128, NCH, 512], BF16) for i in range(2)]
            ys = [sbt("u2_ys%d" % i, [128, 512], F32) for i in range(3)]
            for c in range(2 * NCH):
                kk = c % 2
                hs_ = slice((c % 2) * 1024, (c % 2 + 1) * 1024)
                tk.dma('sp', wst[kk][:], self.w_out[l, (c // 2) * 128:(c // 2 + 1) * 128, hs_], writes=[('uwst', kk)])
                self.pool(lambda: nc.gpsimd.tensor_copy(wo[:, c // 2, hs_], wst[kk][:]), reads=[('uwst', kk)], writes=['uw'])

            def load_m(tt):
                tk.dma('sp', mt[tt % 2][:], mT_v[:, :, tt * 512:(tt + 1) * 512], writes=[('um', tt % 2)])

            load_m(0)
            bi = 0
            yi = 0
            for tt in range(16):
                tp = tt % 2
                ts_ = slice(tt * 512, (tt + 1) * 512)
                if tt + 1 < 16:
                    load_m(tt + 1)
                for ec in range(NCH):
                    bk = bi % 4
                    bi += 1
                    for dc in range(NCH):
                        self.pe(lambda: nc.tensor.matmul(self.ps[bk][:, :], lhsT=wo[:, dc, ec * 128:(ec + 1) * 128], rhs=mt[tp][:, dc, :],
                                                         start=(dc == 0), stop=(dc == NCH - 1)),
                                reads=['uw', ('um', tp)], writes=[('ps', bk)], sig=(dc == NCH - 1))
                    yk = yi % 3
                    yi += 1
                    self.act(lambda: nc.scalar.copy(out=ys[yk][:], in_=self.ps[bk][:, :]), reads=[('ps', bk)], writes=[('uys', yk)])
                    tk.dma('pool', self.yT[tt, ec * 128:(ec + 1) * 128, :], ys[yk][:], reads=[('uys', yk)], writes=[('yT', tt)])
                tk.cc(lambda: nc.gpsimd.collective_compute("AllReduce", ALU.add, replica_groups=[[0, 1], [2, 3], [4, 5], [6, 7]],
                                                           ins=[self.yT[tt].opt()], outs=[self.yTs[tt].opt()]),
                      reads=[('yT', tt)], writes=[('yTs', tt)])
            tk.barrier()

    def phase_ln(self, l, final):
        nc, tk = self.nc, self.tk
        TL = 256
        xTf_v = self.xTf.rearrange("(c p) t -> p c t", p=128)
        xTb_v = self.xTb.rearrange("(c p) t -> p c t", p=128)
        y_v = self.yTs.rearrange("n (c p) t -> n p c t", p=128)
        with contextlib.ExitStack() as es:
            sbt = self._sbt(es)
            xs = [sbt("l_x%d" % i, [128, NCH, TL], F32) for i in range(2)]
            yy = [sbt("l_y%d" % i, [128, NCH, TL], F32) for i in range(2)]
            sq = sbt("l_sq", [128, NCH, TL], F32)
            xb = [sbt("l_xb%d" % i, [128, NCH, TL], BF16) for i in range(2)]
            st = [sbt("l_st%d" % i, [128, 6, TL], F32) for i in range(2)]
            ot = [sbt("l_ot%d" % i, [128, D], F32) for i in range(2)]
            g = sbt("l_g", [128, NCH], F32)
            b = sbt("l_b", [128, NCH], F32)
            tk.dma('sp', g[:], self.lng[l], writes=['lg'])
            tk.dma('sp', b[:], self.lnb[l], writes=['lb'])
            ntile = S // TL

            def load(i):
                p_ = i % 2
                ts_ = slice(i * TL, (i + 1) * TL)
                tk.dma('sp', xs[p_][:], xTf_v[:, :, ts_], writes=[('lx', p_)])
                tk.dma('sp', yy[p_][:], y_v[(i * TL) // 512][:, :, (i * TL) % 512:(i * TL) % 512 + TL], writes=[('ly', p_)])

            load(0)
            oi = 0
            for i in range(ntile):
                p_ = i % 2
                ts_ = slice(i * TL, (i + 1) * TL)
                if i + 1 < ntile:
                    load(i + 1)
                X, Y, ST = xs[p_], yy[p_], st[p_]
                Xf = X[:].rearrange("p c t -> p (c t)")
                Yf = Y[:].rearrange("p c t -> p (c t)")
                self.dve(lambda: nc.vector.scalar_tensor_tensor(out=Xf, in0=Xf, scalar=ALPHA, in1=Yf, op0=ALU.mult, op1=ALU.add),
                         reads=[('lx', p_), ('ly', p_)], writes=[('lx', p_)])
                self.act(lambda: nc.scalar.activation(out=sq[:].rearrange("p c t -> p (c t)"), in_=Xf, func=AF.Square),
                         reads=[('lx', p_)], writes=['lsq'])
                b1, b2 = (i % 2), 2 + (i % 2)
                for c in range(NCH):
                    self.pe(lambda: nc.tensor.matmul(self.ps[b1][:, 0:TL], lhsT=self.ones, rhs=X[:, c, :], start=(c == 0), stop=(c == NCH - 1)),
                            reads=['c_f32', ('lx', p_)], writes=[('ps', b1)], sig=(c == NCH - 1))
                for c in range(NCH):
                    self.pe(lambda: nc.tensor.matmul(self.ps[b2][:, 0:TL], lhsT=self.ones, rhs=sq[:, c, :], start=(c == 0), stop=(c == NCH - 1)),
                            reads=['c_f32', 'lsq'], writes=[('ps', b2)], sig=(c == NCH - 1))
                mean, msq, var, rstd = ST[:, 0, :], ST[:, 1, :], ST[:, 2, :], ST[:, 3, :]
                self.dve(lambda: nc.vector.tensor_scalar(out=mean, in0=self.ps[b1][:, 0:TL], scalar1=1.0 / D, scalar2=None, op0=ALU.mult),
                         reads=[('ps', b1)], writes=[('lst', p_)])
                self.dve(lambda: nc.vector.tensor_tensor(out=msq, in0=mean, in1=mean, op=ALU.mult),
                         reads=[('lst', p_)], writes=[('lst', p_)])
                self.dve(lambda: nc.vector.scalar_tensor_tensor(out=var, in0=self.ps[b2][:, 0:TL], scalar=1.0 / D, in1=msq, op0=ALU.mult, op1=ALU.subtract),
                         reads=[('ps', b2), ('lst', p_)], writes=[('lst', p_)])
                self.dve(lambda: nc.vector.tensor_scalar(out=var, in0=var, scalar1=LN_EPS, scalar2=None, op0=ALU.add),
                         reads=[('lst', p_)], writes=[('lst', p_)])
                self.act(lambda: nc.scalar.activation(out=var, in_=var, func=AF.Sqrt), reads=[('lst', p_)], writes=[('lst', p_)])
                self.dve(lambda: nc.vector.reciprocal(rstd, var), reads=[('lst', p_)], writes=[('lst', p_)])
                mean_b = mean.unsqueeze(1).to_broadcast([128, NCH, TL])
                rstd_b = rstd.unsqueeze(1).to_broadcast([128, NCH, TL])
                self.dve(lambda: nc.vector.tensor_tensor(out=X[:], in0=X[:], in1=mean_b, op=ALU.subtract),
                         reads=[('lx', p_), ('lst', p_)], writes=[('lx', p_)])
                self.pool(lambda: nc.gpsimd.tensor_tensor(out=X[:], in0=X[:], in1=rstd_b, op=ALU.mult),
                          reads=[('lx', p_), ('lst', p_)], writes=[('lx', p_)])
                for c in range(NCH):
                    self.act(lambda: nc.scalar.activation(out=X[:, c, :], in_=X[:, c, :], func=AF.Identity, scale=g[:, c:c + 1], bias=b[:, c:c + 1]),
                             reads=[('lx', p_), 'lg', 'lb'], writes=[('lx', p_)])
                if not final:
                    self.pool(lambda: nc.gpsimd.tensor_copy(xb[p_][:], X[:]), reads=[('lx', p_)], writes=[('lxb', p_)])
                    tk.dma('pool', xTf_v[:, :, ts_], X[:], reads=[('lx', p_)])
                    tk.dma('pool', xTb_v[:, :, ts_], xb[p_][:], reads=[('lxb', p_)])
                else:
                    for sub in range(TL // 128):
                        ok = oi % 2
                        oi += 1
                        for cg in range(4):
                            bk = 4 + (oi * 4 + cg) % 4
                            bank = self.ps[bk]
                            for c4 in range(4):
                                c = cg * 4 + c4
                                self.pe(lambda: nc.tensor.transpose(out=bank[:, c4 * 128:(c4 + 1) * 128], in_=X[:, c, sub * 128:(sub + 1) * 128], identity=self.ident),
                                        reads=[('lx', p_), 'c_f32'], writes=[('ps', bk)], sig=(c4 == 3))
                            self.act(lambda: nc.scalar.copy(out=ot[ok][:, cg * 512:(cg + 1) * 512], in_=bank[:, :]),
                                     reads=[('ps', bk)], writes=[('lot', ok)])
                        r0 = i * TL + sub * 128
                        tk.dma('pool', self.out[r0:r0 + 128, :], ot[ok][:], reads=[('lot', ok)])
            tk.barrier()

    def build(self, phases=None):
        self.load_consts()
        self.phase_x0()
        for l in range(self.n_layers):
            self.phase_p(l)
            self.phase_a()
            self.phase_b()
            self.phase_u(l)
            self.phase_ln(l, final=(l == self.n_layers - 1))
        self.tk.barrier()
        return self.nc


def _consts():
    p = np.arange(128)
    ident = np.eye(128, dtype=np.float32)
    ones = np.ones((128, 128), np.float32)
    tri = (p[:, None] <= p[None, :]).astype(np.float32)
    perm = np.zeros((128, 128), np.float32)
    perm[(p + 64) % 128, p] = 1.0
    cf32 = np.concatenate([ident, ones, tri, perm], axis=1)
    q = np.arange(512)
    fm = [np.where(q[None, :] >= 128 * j + p[:, None], 0.0, NEG).astype(np.float32) for j in range(4)]
    q1 = np.arange(128)
    m_prev = np.where(q1[None, :] <= p[:, None], 0.0, NEG).astype(np.float32)
    m_diag = np.where(p[:, None] <= q1[None, :], 0.0, NEG).astype(np.float32)
    cbf = np.concatenate([ident, ones] + fm + [m_prev, m_diag], axis=1).astype(ml_dtypes.bfloat16)
    half = HD // 2
    inv_freq = (np.float32(10000.0) ** (-np.arange(half, dtype=np.float32) / np.float32(half))).astype(np.float32)
    ang = np.arange(S, dtype=np.float32)[None, :] * inv_freq[:, None]
    cos = np.cos(ang).astype(np.float32)
    sin = np.sin(ang).astype(np.float32)
    cosT = np.concatenate([cos, cos], axis=0)
    sinT = np.concatenate([-sin, sin], axis=0)
    return cf32, cbf, np.ascontiguousarray(cosT), np.ascontiguousarray(sinT)


def _col_index(p):
    blk = NHA * HD
    hsel = np.arange(p * W, (p + 1) * W)
    idx = [b * blk + hsel for b in (0, 1, 3, 4, 5, 7)]
    g0 = 8 * blk + NHA
    idx.append(np.arange(g0, g0 + 2 * D))
    idx.append(2 * blk + hsel)
    idx.append(6 * blk + hsel)
    idx.append(8 * blk + np.arange(p * NH, (p + 1) * NH))
    return np.concatenate(idx)


def make_in_maps(inputs, n_layers=DEPTH, cores=8):
    x = np.asarray(inputs["x"], np.float32)
    cf32, cbf, cosT, sinT = _consts()
    common = dict(
        w_out=np.ascontiguousarray(np.asarray(inputs["w_out"], np.float32)[:n_layers]),
        bgt=np.ascontiguousarray(np.asarray(inputs["b_gate"], np.float32)[:n_layers].reshape(n_layers, 32, 128).transpose(0, 2, 1)),
        lng=np.ascontiguousarray(np.asarray(inputs["ln_g"], np.float32)[:n_layers].reshape(n_layers, NCH, 128).transpose(0, 2, 1)),
        lnb=np.ascontiguousarray(np.asarray(inputs["ln_b"], np.float32)[:n_layers].reshape(n_layers, NCH, 128).transpose(0, 2, 1)),
        cosT=cosT, sinT=sinT, cf32=cf32, cbf=cbf)
    half = {}
    for p in (0, 1):
        hsel = slice(p * W, (p + 1) * W)
        half[p] = dict(
            w_sel=np.ascontiguousarray(np.asarray(inputs["w_in"], np.float32)[:n_layers][:, :, _col_index(p)]),
            w_upa=np.ascontiguousarray(np.asarray(inputs["w_up_a"], np.float32)[:n_layers, hsel, :]),
            w_upb=np.ascontiguousarray(np.asarray(inputs["w_up_b"], np.float32)[:n_layers, hsel, :]),
            bfr=np.ascontiguousarray(np.broadcast_to(
                np.asarray(inputs["b_forget"], np.float32)[:n_layers, None, p * NH:(p + 1) * NH], (n_layers, 128, NH))))
    maps = []
    for core in range(cores):
        b, p = core // 2, core % 2
        m = dict(common)
        m.update(half[p])
        m["x"] = np.ascontiguousarray(x[b])
        maps.append(m)
    return maps


def kernel(x, w_in, b_forget, b_gate, w_up_a, w_up_b, w_out, ln_g, ln_b):
    inputs = dict(x=x, w_in=w_in, b_forget=b_forget, b_gate=b_gate, w_up_a=w_up_a, w_up_b=w_up_b,
                  w_out=w_out, ln_g=ln_g, ln_b=ln_b)
    bld = Builder(n_layers=DEPTH)
    nc = bld.build()
    in_maps = make_in_maps(inputs)
    res = run_bass_kernel_spmd(nc, in_maps, core_ids=list(range(8)))
    return np.stack([res.results[2 * b]["out"] for b in range(4)], axis=0).astype(np.float32)
```

```python
import contextlib
import numpy as np
import ml_dtypes
import concourse.bass as bass
import concourse.mybir as mybir
from concourse.bass_utils import run_bass_kernel_spmd

F32 = mybir.dt.float32
BF16 = mybir.dt.bfloat16
AF = mybir.ActivationFunctionType
ALU = mybir.AluOpType

D = 2048
S = 8192
DEPTH = 4
HD = 128
NHA = 12
NH = 6
W = NH * HD
WB = 768
NCH = D // 128
NEG = -30000.0
ALPHA = float((2 * DEPTH) ** 0.25)
LN_EPS = 1e-5
SCALE = HD ** -0.5
PATTERNS = ((128, 1), (512, 4), (2048, 16))
NCOL = 8 * W + 4096 + NH
C_QA, C_KA, C_ZA, C_QB, C_KB, C_ZB, C_G, C_VA, C_VB = 0, W, 2 * W, 3 * W, 4 * W, 5 * W, 6 * W, 6 * W + 4096, 7 * W + 4096


class TK:
    def __init__(self, nc):
        self.nc = nc
        self.eng = {'pe': nc.tensor, 'act': nc.scalar, 'dve': nc.vector, 'pool': nc.gpsimd, 'sp': nc.sync}
        self.semh = {}
        self.cnt = {}
        for k in ('pe', 'act', 'dve', 'pool'):
            self.semh[k] = nc.alloc_semaphore('s_' + k)
            self.cnt[k] = 0
        self.semh['cc'] = nc.alloc_semaphore('s_cc')
        self.ncc = 0
        self.pending = {k: False for k in self.cnt}
        self.waited = {k: {} for k in self.eng}
        self.lastw = {}
        self.readers = {}
        self.dq = {}
        for q in ('sp', 'pool', 'act'):
            sems = []
            for i in range(8):
                name = 'd_%s%d' % (q, i)
                self.semh[name] = nc.alloc_semaphore(name)
                sems.append(name)
            self.dq[q] = dict(eng=q, sems=sems, i=0, ev=[None] * 8)

    def _wait(self, e, ev):
        if ev is None:
            return
        sk, val = ev
        if e == 'pe' and sk == 'pe':
            return
        w = self.waited[e]
        if w.get(sk, 0) >= val:
            return
        self.eng[e].wait_ge(self.semh[sk], val)
        w[sk] = val

    def _deps(self, e, reads, writes):
        for k in reads:
            self._wait(e, self.lastw.get(k))
        for k in writes:
            self._wait(e, self.lastw.get(k))
            r = self.readers.get(k)
            if r:
                for sk, val in r.items():
                    self._wait(e, (sk, val))

    def _commit(self, ev, reads, writes):
        for k in writes:
            self.lastw[k] = ev
            self.readers[k] = {}
        for k in reads:
            r = self.readers.setdefault(k, {})
            if r.get(ev[0], 0) < ev[1]:
                r[ev[0]] = ev[1]

    def op(self, e, fn, reads=(), writes=(), sig=True):
        self._deps(e, reads, writes)
        ins = fn()
        if sig:
            self.cnt[e] += 1
            ins.then_inc(self.semh[e], 1)
            self.pending[e] = False
            ev = (e, self.cnt[e])
        else:
            ev = (e, self.cnt[e] + 1)
            self.pending[e] = True
        self._commit(ev, reads, writes)
        return ev

    def dma(self, q, out, in_, reads=(), writes=()):
        Q = self.dq[q]
        i = Q['i']
        slot = i % 8
        e = Q['eng']
        self._wait(e, Q['ev'][slot])
        self._deps(e, reads, writes)
        sk = Q['sems'][slot]
        self.eng[e].dma_start(out=out, in_=in_).then_inc(self.semh[sk], 16)
        ev = (sk, 16 * (i // 8 + 1))
        Q['ev'][slot] = ev
        Q['i'] = i + 1
        self._commit(ev, reads, writes)
        return ev

    def cc(self, fn, reads=(), writes=()):
        self._deps('pool', reads, writes)
        fn().then_inc(self.semh['cc'])
        self.ncc += 1
        ev = ('cc', self.ncc)
        self._commit(ev, reads, writes)
        return ev

    def barrier(self, engines=('pe', 'act', 'dve', 'pool', 'sp'), keep=None, wait_cc=True):
        assert not any(self.pending.values()), self.pending
        evs = [(k, self.cnt[k]) for k in self.cnt if self.cnt[k] > 0]
        if self.ncc and wait_cc:
            evs.append(('cc', self.ncc))
        for Q in self.dq.values():
            evs += [ev for ev in Q['ev'] if ev is not None]
        for e in engines:
            for ev in evs:
                if e == ev[0]:
                    continue
                self._wait(e, ev)
        self.lastw = {k: v for k, v in self.lastw.items() if keep and isinstance(k, tuple) and k[0] == keep}
        self.readers = {}


class Builder:
    def __init__(self, n_layers=DEPTH, debug=(), collective=True, groups=None):
        self.n_layers = n_layers
        self.debug = debug
        self.collective = collective
        self.groups = groups or [[0, 1], [2, 3], [4, 5], [6, 7]]
        nc = self.nc = bass.Bass("TRN2", target_bir_lowering=False)
        self.tk = TK(nc)
        dt0 = nc.dram_tensor

        def dt(name, shape, dtype, kind):
            if kind is None:
                kind = "ExternalOutput" if name in debug else "Internal"
            return dt0(name, shape, dtype, kind=kind)
        kind_s = None
        self.x = dt("x", [S, D], F32, kind="ExternalInput").ap()
        self.w_sel = dt("w_sel", [n_layers, D, NCOL], F32, kind="ExternalInput").ap()
        self.w_upa = dt("w_upa", [n_layers, W, D], F32, kind="ExternalInput").ap()
        self.w_upb = dt("w_upb", [n_layers, W, D], F32, kind="ExternalInput").ap()
        self.w_out = dt("w_out", [n_layers, D, D], F32, kind="ExternalInput").ap()
        self.bfr = dt("bfr", [n_layers, 128, NH], F32, kind="ExternalInput").ap()
        self.bgt = dt("bgt", [n_layers, 128, 32], F32, kind="ExternalInput").ap()
        self.lng = dt("lng", [n_layers, 128, NCH], F32, kind="ExternalInput").ap()
        self.lnb = dt("lnb", [n_layers, 128, NCH], F32, kind="ExternalInput").ap()
        self.cosT = dt("cosT", [128, S], F32, kind="ExternalInput").ap()
        self.sinT = dt("sinT", [128, S], F32, kind="ExternalInput").ap()
        self.cf32 = dt("cf32", [128, 4 * 128], F32, kind="ExternalInput").ap()
        self.cbf = dt("cbf", [128, 2 * 128 + 4 * 512 + 256], BF16, kind="ExternalInput").ap()
        self.out = dt("out", [S, D], F32, kind="ExternalOutput").ap()
        self.xTf = dt("xTf", [D, S], F32, kind=kind_s).ap()
        self.xTb = dt("xTb", [D, S], BF16, kind=kind_s).ap()
        self.QAT = dt("QAT", [W, S], BF16, kind=kind_s).ap()
        self.KAT = dt("KAT", [W, S], BF16, kind=kind_s).ap()
        self.ZAT = dt("ZAT", [W, S], BF16, kind=kind_s).ap()
        self.QBT = dt("QBT", [W, S], BF16, kind=kind_s).ap()
        self.KBT = dt("KBT", [W, S], BF16, kind=kind_s).ap()
        self.ZBT = dt("ZBT", [W, S], BF16, kind=kind_s).ap()
        self.VA = dt("VA", [S, W], BF16, kind=kind_s).ap()
        self.VB = dt("VB", [S, W], BF16, kind=kind_s).ap()
        self.GT = dt("GT", [4096, S], BF16, kind=kind_s).ap()
        self.GAT = dt("GAT", [W, S], BF16, kind=kind_s).ap()
        self.GBT = dt("GBT", [W, S], BF16, kind=kind_s).ap()
        self.yT = dt("yT", [16, D, 512], F32, kind=kind_s).ap()
        self.yTs = dt("yTs", [16, D, 512], F32, kind=kind_s).ap()
        self.mT = dt("mT", [D, S], BF16, kind=kind_s).ap()
        sb = nc.alloc_sbuf_tensor
        self.c_f32 = sb("c_f32", [128, 512], F32)
        self.c_bf = sb("c_bf", [128, 2 * 128 + 4 * 512 + 256], BF16)
        self.LU = sb("LU", [128, 64, NH], F32)
        self.cumL = sb("cumL", [128, 64, NH], F32)
        self.carry = sb("carry", [128, 65, NH], F32)
        self.ps = [nc.alloc_psum_tensor("ps%d" % i, [128, 512], F32) for i in range(8)]
        self.ident = self.c_f32[:, 0:128]
        self.ones = self.c_f32[:, 128:256]
        self.tri = self.c_f32[:, 256:384]
        self.perm = self.c_f32[:, 384:512]
        self.ident_b = self.c_bf[:, 0:128]
        self.ones_b = self.c_bf[:, 128:256]
        self.fmask = [self.c_bf[:, 256 + j * 512: 256 + (j + 1) * 512] for j in range(4)]
        self.dmask = self.c_bf[:, 256 + 2048: 256 + 2048 + 256]

    def _sbt(self, es):
        self.uid = getattr(self, 'uid', 0) + 1
        u = self.uid
        return lambda n, s, d: es.enter_context(self.nc.sbuf_tensor("%s_%d" % (n, u), s, d))

    def pe(self, fn, reads=(), writes=(), sig=True):
        return self.tk.op('pe', fn, reads, writes, sig)

    def act(self, fn, reads=(), writes=()):
        return self.tk.op('act', fn, reads, writes)

    def dve(self, fn, reads=(), writes=()):
        return self.tk.op('dve', fn, reads, writes)

    def pool(self, fn, reads=(), writes=()):
        return self.tk.op('pool', fn, reads, writes)

    def load_consts(self):
        self.tk.dma('sp', self.c_f32[:], self.cf32[:, :], writes=['c_f32'])
        self.tk.dma('sp', self.c_bf[:], self.cbf[:, :], writes=['c_bf'])

    def phase_x0(self):
        nc, tk = self.nc, self.tk
        xTf_v = self.xTf.rearrange("(c p) t -> p c t", p=128)
        xTb_v = self.xTb.rearrange("(c p) t -> p c t", p=128)
        with contextlib.ExitStack() as es:
            sbt = self._sbt(es)
            xin = [sbt("x0_in%d" % i, [128, D], F32) for i in range(2)]
            stf = [sbt("x0_sf%d" % i, [128, NCH, 512], F32) for i in range(2)]
            stb = [sbt("x0_sb%d" % i, [128, NCH, 512], BF16) for i in range(2)]
            g = 0
            for tt in range(16):
                for sub in range(4):
                    i = tt * 4 + sub
                    xi = xin[i % 2]
                    tk.dma('sp', xi[:], self.x[i * 128:(i + 1) * 128, :], writes=[('xin', i % 2)])
                    for cg in range(4):
                        bk = g % 4
                        g += 1
                        bank = self.ps[bk]
                        for c4 in range(4):
                            c = cg * 4 + c4
                            self.pe(lambda: nc.tensor.transpose(out=bank[:, c4 * 128:(c4 + 1) * 128],
                                                                in_=xi[:, c * 128:(c + 1) * 128], identity=self.ident),
                                    reads=[('xin', i % 2), 'c_f32'], writes=[('ps', bk)], sig=(c4 == 3))
                        src = bank[:, :].rearrange("p (c t) -> p c t", c=4)
                        self.act(lambda: nc.scalar.copy(out=stf[tt % 2][:, cg * 4:cg * 4 + 4, sub * 128:(sub + 1) * 128], in_=src),
                                 reads=[('ps', bk)], writes=[('stf', tt % 2)])
                        self.pool(lambda: nc.gpsimd.tensor_copy(stb[tt % 2][:, cg * 4:cg * 4 + 4, sub * 128:(sub + 1) * 128],
                                                                stf[tt % 2][:, cg * 4:cg * 4 + 4, sub * 128:(sub + 1) * 128]),
                                  reads=[('stf', tt % 2)], writes=[('stb', tt % 2)])
                tk.dma('pool', xTf_v[:, :, tt * 512:(tt + 1) * 512], stf[tt % 2][:], reads=[('stf', tt % 2)])
                tk.dma('pool', xTb_v[:, :, tt * 512:(tt + 1) * 512], stb[tt % 2][:], reads=[('stb', tt % 2)])
            tk.barrier()

    def phase_p(self, l):
        nc, tk = self.nc, self.tk
        xTb_v = self.xTb.rearrange("(c p) t -> p c t", p=128)
        blocks = []
        for (kind, dstT, c0) in (('rope', self.QAT, C_QA), ('rope', self.KAT, C_KA), ('silu', self.ZAT, C_ZA),
                                 ('copy', self.QBT, C_QB), ('copy', self.KBT, C_KB), ('silu', self.ZBT, C_ZB)):
            for hb in range(W // WB):
                blocks.append((kind, dstT[hb * WB:(hb + 1) * WB, :], c0 + hb * WB, WB))
        for gi in range(4):
            blocks.append(('gate', self.GT[gi * 1024:(gi + 1) * 1024, :], C_G + gi * 1024, 1024))
        nvb = W // WB
        for hb in range(nvb):
            blocks.append(('v', self.VA[:, hb * WB:(hb + 1) * WB], C_VA + hb * WB, WB))
        for hb in range(nvb):
            last = hb == nvb - 1
            blocks.append(('vf' if last else 'v', self.VB[:, hb * WB:(hb + 1) * WB], C_VB + hb * WB, WB + (NH if last else 0)))
        with contextlib.ExitStack() as es:
            sbt = self._sbt(es)
            wst = [sbt("p_wst%d" % i, [128, 1024], F32) for i in range(4)]
            wbf = [sbt("p_wbf%d" % i, [128, NCH, 1024], BF16) for i in range(2)]
            xt = [sbt("p_xt%d" % i, [128, NCH, 512], BF16) for i in range(3)]
            cs = [sbt("p_cos%d" % i, [128, 512], F32) for i in range(2)]
            sn = [sbt("p_sin%d" % i, [128, 512], F32) for i in range(2)]
            stg = [sbt("p_stg%d" % i, [128, 8, 512], BF16) for i in range(2)]
            stv = [sbt("p_stv%d" % i, [128, 4, WB], BF16) for i in range(2)]
            qf = [sbt("p_qf%d" % i, [128, 512], F32) for i in range(3)]
            t1 = [sbt("p_t1%d" % i, [128, 512], F32) for i in range(2)]
            t2 = [sbt("p_t2%d" % i, [128, 512], F32) for i in range(2)]
            bg = sbt("p_bg", [128, 32], F32)
            bf = sbt("p_bf", [128, NH], F32)
            tk.dma('sp', bg[:], self.bgt[l], writes=['bg'])
            tk.dma('sp', bf[:], self.bfr[l], writes=['bf'])
            wcount = [0]
            xcount = [0]
            gcount = [0]
            rcount = [0]
            scount = [0]

            def load_w(bi):
                kind, dst, c0, ncols = blocks[bi]
                par = bi % 2
                for dch in range(NCH):
                    k = wcount[0] % 4
                    wcount[0] += 1
                    tk.dma('sp', wst[k][:, 0:ncols], self.w_sel[l, dch * 128:(dch + 1) * 128, c0:c0 + ncols],
                           writes=[('wst', k)])
                    self.pool(lambda: nc.gpsimd.tensor_copy(wbf[par][:, dch, 0:ncols], wst[k][:, 0:ncols]),
                              reads=[('wst', k)], writes=[('wbf', par)])

            def load_x(tt, rope):
                k = xcount[0] % 3
                xcount[0] += 1
                tk.dma('sp', xt[k][:], xTb_v[:, :, tt * 512:(tt + 1) * 512], writes=[('xt', k)])
                if rope:
                    tk.dma('sp', cs[tt % 2][:], self.cosT[:, tt * 512:(tt + 1) * 512], writes=[('cs', tt % 2)])
                    tk.dma('sp', sn[tt % 2][:], self.sinT[:, tt * 512:(tt + 1) * 512], writes=[('sn', tt % 2)])
                return k

            load_w(0)
            for bi, (kind, dst, c0, ncols) in enumerate(blocks):
                par = bi % 2
                rope = kind == 'rope'
                xk_next = load_x(0, rope)
                if bi + 1 < len(blocks):
                    load_w(bi + 1)
                for tt in range(16):
                    xk = xk_next
                    if tt + 1 < 16:
                        xk_next = load_x(tt + 1, rope)
                    X = xt[xk]
                    if kind in ('v', 'vf'):
                        sv = scount[0] % 2
                        scount[0] += 1
                        for sub in range(4):
                            for (cc0, cn) in ((0, 512), (512, ncols - 512)):
                                bk = gcount[0] % 4
                                gcount[0] += 1
                                bank = self.ps[bk]
                                for dch in range(NCH):
                                    self.pe(lambda: nc.tensor.matmul(bank[:, 0:cn], lhsT=X[:, dch, sub * 128:(sub + 1) * 128],
                                                                     rhs=wbf[par][:, dch, cc0:cc0 + cn],
                                                                     start=(dch == 0), stop=(dch == NCH - 1)),
                                            reads=[('xt', xk), ('wbf', par)], writes=[('ps', bk)], sig=(dch == NCH - 1))
                                nv = min(cn, WB - cc0)
                                if kind == 'vf' and cc0 == 512:
                                    self.dve(lambda: nc.vector.tensor_copy(stv[sv][:, sub, cc0:cc0 + nv], bank[:, 0:nv]),
                                             reads=[('ps', bk)], writes=[('stv', sv)])
                                else:
                                    self.act(lambda: nc.scalar.copy(out=stv[sv][:, sub, cc0:cc0 + nv], in_=bank[:, 0:nv]),
                                             reads=[('ps', bk)], writes=[('stv', sv)])
                                if kind == 'vf' and cc0 == 512:
                                    self.dve(lambda: nc.vector.tensor_tensor(out=self.LU[:, tt * 4 + sub, :], in0=bank[:, nv:nv + NH],
                                                                             in1=bf[:], op=ALU.add),
                                             reads=[('ps', bk), 'bf'], writes=['LU'])
                        dv = dst[tt * 512:(tt + 1) * 512, :].rearrange("(n p) c -> p n c", p=128)
                        tk.dma('pool', dv, stv[sv][:], reads=[('stv', sv)])
                        continue
                    nchunk = ncols // 128
                    sg = scount[0] % 2
                    scount[0] += 1
                    pend = []
                    for oc in range(nchunk):
                        bk = gcount[0] % 4
                        gcount[0] += 1
                        bank = self.ps[bk]
                        for dch in range(NCH):
                            self.pe(lambda: nc.tensor.matmul(bank[:, :], lhsT=wbf[par][:, dch, oc * 128:(oc + 1) * 128],
                                                             rhs=X[:, dch, :], start=(dch == 0), stop=(dch == NCH - 1)),
                                    reads=[('xt', xk), ('wbf', par)], writes=[('ps', bk)], sig=(dch == NCH - 1))
                        if kind == 'copy':
                            self.act(lambda: nc.scalar.copy(out=stg[sg][:, oc, :], in_=bank[:, :]),
                                     reads=[('ps', bk)], writes=[('stg', sg)])
                        elif kind == 'silu':
                            self.act(lambda: nc.scalar.activation(out=stg[sg][:, oc, :], in_=bank[:, :], func=AF.Silu),
                                     reads=[('ps', bk)], writes=[('stg', sg)])
                        elif kind == 'gate':
                            gc = (c0 - C_G) // 128 + oc
                            self.act(lambda: nc.scalar.activation(out=stg[sg][:, oc, :], in_=bank[:, :], func=AF.Sigmoid,
                                                                  bias=bg[:, gc:gc + 1], scale=1.0),
                                     reads=[('ps', bk), 'bg'], writes=[('stg', sg)])
                        else:
                            qk = rcount[0] % 3
                            rcount[0] += 1
                            self.act(lambda: nc.scalar.copy(out=qf[qk][:], in_=bank[:, :]),
                                     reads=[('ps', bk)], writes=[('qf', qk)])
                            pend.append((oc, qk))
                            if len(pend) > 1:
                                self._rope_finish(pend.pop(0), tt, sg, stg, qf, t1, t2, cs, sn)
                    while pend:
                        self._rope_finish(pend.pop(0), tt, sg, stg, qf, t1, t2, cs, sn)
                    dv = dst.rearrange("(c p) t -> p c t", p=128)[:, :, tt * 512:(tt + 1) * 512]
                    tk.dma('pool', dv, stg[sg][:, 0:nchunk, :], reads=[('stg', sg)])
            LUf = self.LU[:].rearrange("p k h -> p (k h)")
            self.act(lambda: nc.scalar.activation(out=LUf, in_=LUf, func=AF.Exp, scale=-1.0), reads=['LU'], writes=['LU'])
            self.act(lambda: nc.scalar.activation(out=LUf, in_=LUf, func=AF.Ln, bias=1.0, scale=1.0), reads=['LU'], writes=['LU'])
            n = 32 * NH
            tot = sbt("p_tot", [128, 64, NH], F32)
            totf = tot[:].rearrange("p k h -> p (k h)")
            for hf in range(2):
                self.pe(lambda: nc.tensor.matmul(self.ps[hf][:, 0:n], lhsT=self.tri, rhs=LUf[:, hf * n:(hf + 1) * n], start=True, stop=True),
                        reads=['LU', 'c_f32'], writes=[('ps', hf)])
                self.pe(lambda: nc.tensor.matmul(self.ps[2 + hf][:, 0:n], lhsT=self.ones, rhs=LUf[:, hf * n:(hf + 1) * n], start=True, stop=True),
                        reads=['LU', 'c_f32'], writes=[('ps', 2 + hf)])
                self.dve(lambda: nc.vector.tensor_copy(totf[:, hf * n:(hf + 1) * n], self.ps[2 + hf][:, 0:n]),
                         reads=[('ps', 2 + hf)], writes=['tot'])
            self.dve(lambda: nc.vector.memset(self.carry[:, 0, :], 0.0), writes=['carry'])
            for kt in range(64):
                self.dve(lambda: nc.vector.tensor_tensor(out=self.carry[:, kt + 1, :], in0=self.carry[:, kt, :], in1=tot[:, kt, :], op=ALU.add),
                         reads=['carry', 'tot'], writes=['carry'])
            cumf = self.cumL[:].rearrange("p k h -> p (k h)")
            carf = self.carry[:, 0:64, :].rearrange("p k h -> p (k h)")
            for hf in range(2):
                self.dve(lambda: nc.vector.tensor_tensor(out=cumf[:, hf * n:(hf + 1) * n], in0=self.ps[hf][:, 0:n],
                                                         in1=carf[:, hf * n:(hf + 1) * n], op=ALU.add),
                         reads=[('ps', hf), 'carry'], writes=['cumL'])
            tk.barrier()

    def _rope_finish(self, item, tt, sg, stg, qf, t1, t2, cs, sn):
        nc = self.nc
        oc, qk = item
        bk = 4 + (oc % 2)
        bank = self.ps[bk]
        k2 = oc % 2
        self.pe(lambda: nc.tensor.matmul(bank[:, :], lhsT=self.perm, rhs=qf[qk][:], start=True, stop=True),
                reads=[('qf', qk), 'c_f32'], writes=[('ps', bk)])
        self.pool(lambda: nc.gpsimd.tensor_tensor(out=t1[k2][:], in0=qf[qk][:], in1=cs[tt % 2][:], op=ALU.mult),
                  reads=[('qf', qk), ('cs', tt % 2)], writes=[('t1', k2)])
        self.dve(lambda: nc.vector.tensor_tensor(out=t2[k2][:], in0=bank[:, :], in1=sn[tt % 2][:], op=ALU.mult),
                 reads=[('ps', bk), ('sn', tt % 2)], writes=[('t2', k2)])
        self.dve(lambda: nc.vector.tensor_tensor(out=stg[sg][:, oc, :], in0=t1[k2][:], in1=t2[k2][:], op=ALU.add),
                 reads=[('t1', k2), ('t2', k2)], writes=[('stg', sg)])

    def phase_a(self):
        nc, tk = self.nc, self.tk
        with contextlib.ExitStack() as es:
            sbt = self._sbt(es)
            QT = sbt("a_q", [128, S], BF16)
            KT = sbt("a_k", [128, S], BF16)
            V = [sbt("a_v%d" % i, [128, 64, HD], BF16) for i in range(3)]
            accn = sbt("a_accn", [128, S], F32)
            accd = sbt("a_accd", [128, S], F32)
            PT = [sbt("a_pt%d" % i, [128, 256], BF16) for i in range(4)]
            ZT = [sbt("a_z%d" % i, [128, 2048], BF16) for i in range(2)]
            og = [sbt("a_og%d" % i, [128, 2048], BF16) for i in range(2)]
            git = [0]
            zc = 0
            for h in range(NH):
                hs = slice(h * 128, (h + 1) * 128)
                tk.dma('sp', QT[:], self.QAT[hs, :], writes=['aq'])
                tk.dma('sp', KT[:], self.KAT[hs, :], writes=['ak'])
                for pi, (win, d) in enumerate(PATTERNS):
                    nm = 64 // d
                    src = self.VA[:, hs].rearrange("(m p r) e -> r p m e", p=128, r=d)
                    for r in range(d):
                        tk.dma('sp', V[pi][:, r * nm:(r + 1) * nm, :], src[r], writes=[('av', pi)])
                subs = [(pi, d, 64 // d, r, n) for pi, (win, d) in enumerate(PATTERNS) for r in range(d) for n in range(64 // d)]

                def stage_s(idx):
                    pi, d, nm, r, n = subs[idx]
                    it = git[0] + idx
                    sb_ = it % 3
                    pk = it % 4
                    sbank = self.ps[sb_]
                    qsl = slice(n * 128 * d + r, n * 128 * d + r + 127 * d + 1, d)
                    c0 = 0 if n > 0 else 128
                    self.pe(lambda: nc.tensor.matmul(sbank[:, c0:256], lhsT=self.ident_b, rhs=self.dmask[:, c0:256],
                                                     start=True, stop=False),
                            reads=['c_bf'], writes=[('ps', sb_)], sig=False)
                    for j in ((0, 1) if n > 0 else (1,)):
                        m = n - 1 + j
                        ksl = slice(m * 128 * d + r, m * 128 * d + r + 127 * d + 1, d)
                        self.pe(lambda: nc.tensor.matmul(sbank[:, j * 128:(j + 1) * 128], lhsT=KT[:, ksl], rhs=QT[:, qsl],
                                                         start=False, stop=(j == 1)),
                                reads=['aq', 'ak'], writes=[('ps', sb_)], sig=(j == 1))
                    self.act(lambda: nc.scalar.activation(out=PT[pk][:, c0:256], in_=sbank[:, c0:256], func=AF.Exp, scale=SCALE),
                             reads=[('ps', sb_)], writes=[('apt', pk)])

                def stage_f(idx):
                    pi, d, nm, r, n = subs[idx]
                    it = git[0] + idx
                    pk = it % 4
                    nb, db = 3 + it % 2, 5 + it % 2
                    nbank, dbank = self.ps[nb], self.ps[db]
                    qsl = slice(n * 128 * d + r, n * 128 * d + r + 127 * d + 1, d)
                    js = (0, 1) if n > 0 else (1,)
                    for j in js:
                        m = n - 1 + j
                        self.pe(lambda: nc.tensor.matmul(nbank[:, 0:128], lhsT=V[pi][:, r * nm + m, :], rhs=PT[pk][:, j * 128:(j + 1) * 128],
                                                         start=(j == js[0]), stop=(j == 1)),
                                reads=[('av', pi), ('apt', pk)], writes=[('ps', nb)], sig=(j == 1))
                    for j in js:
                        self.pe(lambda: nc.tensor.matmul(dbank[:, 0:128], lhsT=self.ones_b, rhs=PT[pk][:, j * 128:(j + 1) * 128],
                                                         start=(j == js[0]), stop=(j == 1)),
                                reads=['c_bf', ('apt', pk)], writes=[('ps', db)], sig=(j == 1))
                    kk = idx % 4
                    if pi == 0:
                        extra = ['afin_n', 'afin_d'] if idx == 0 else []
                        self.dve(lambda: nc.vector.tensor_copy(accn[:, qsl], nbank[:, 0:128]),
                                 reads=[('ps', nb)], writes=[('accn', 0, kk)] + extra[:1])
                        self.dve(lambda: nc.vector.tensor_copy(accd[:, qsl], dbank[:, 0:128]),
                                 reads=[('ps', db)], writes=[('accd', 0, kk)] + extra[1:])
                    else:
                        prevn = [('accn', pi - 1, q_) for q_ in range(4)]
                        prevd = [('accd', pi - 1, q_) for q_ in range(4)]
                        self.dve(lambda: nc.vector.tensor_tensor(out=accn[:, qsl], in0=nbank[:, 0:128], in1=accn[:, qsl], op=ALU.add),
                                 reads=[('ps', nb)] + prevn, writes=[('accn', pi, kk)])
                        self.dve(lambda: nc.vector.tensor_tensor(out=accd[:, qsl], in0=dbank[:, 0:128], in1=accd[:, qsl], op=ALU.add),
                                 reads=[('ps', db)] + prevd, writes=[('accd', pi, kk)])

                LA = 2
                ns = len(subs)
                for idx in range(min(LA, ns)):
                    stage_s(idx)
                for idx in range(ns):
                    if idx + LA < ns:
                        stage_s(idx + LA)
                    stage_f(idx)
                git[0] += ns
                lastn = [('accn', 2, q_) for q_ in range(4)]
                lastd = [('accd', 2, q_) for q_ in range(4)]
                for c in range(4):
                    zk = zc % 2
                    zc += 1
                    csl = slice(c * 2048, (c + 1) * 2048)
                    tk.dma('sp', ZT[zk][:], self.ZAT[hs, csl], writes=[('az', zk)])
                    self.dve(lambda: nc.vector.reciprocal(accd[:, csl], accd[:, csl]), reads=lastd, writes=['afin_d'])
                    self.pool(lambda: nc.gpsimd.tensor_tensor(out=accn[:, csl], in0=accn[:, csl], in1=accd[:, csl], op=ALU.mult),
                              reads=lastn + ['afin_d'], writes=['afin_n'])
                    self.pool(lambda: nc.gpsimd.tensor_tensor(out=og[zk][:], in0=accn[:, csl], in1=ZT[zk][:], op=ALU.mult),
                              reads=['afin_n', ('az', zk)], writes=[('aog', zk)])
                    tk.dma('pool', self.GAT[hs, csl], og[zk][:], reads=[('aog', zk)])
            tk.barrier()

    def phase_b(self):
        nc, tk = self.nc, self.tk
        with contextlib.ExitStack() as es:
            sbt = self._sbt(es)
            QT = [sbt("b_q%d" % i, [128, S], BF16) for i in range(2)]
            KT = [sbt("b_k%d" % i, [128, S], BF16) for i in range(2)]
            V = [sbt("b_v%d" % i, [128, 64, HD], BF16) for i in range(2)]
            bias = [sbt("b_bias%d" % i, [128, 64, 32], F32) for i in range(2)]
            PT = [sbt("b_pt%d" % i, [128, 512], BF16) for i in range(6)]
            ZT = [sbt("b_z%d" % i, [128, 512], BF16) for i in range(2)]
            rd = [sbt("b_rd%d" % i, [128, 512], F32) for i in range(2)]
            on = [sbt("b_on%d" % i, [128, 512], F32) for i in range(2)]
            og = [sbt("b_og%d" % i, [128, 512], BF16) for i in range(2)]

            def load_head(h):
                hp = h % 2
                hs = slice(h * 128, (h + 1) * 128)
                tk.dma('sp', QT[hp][:], self.QBT[hs, :], writes=[('bq', hp)])
                tk.dma('sp', KT[hp][:], self.KBT[hs, :], writes=[('bk', hp)])
                tk.dma('sp', V[hp][:], self.VB[:, hs].rearrange("(m p) e -> p m e", p=128), writes=[('bv', hp)])
                for qs in range(32):
                    self.dve(lambda: nc.vector.tensor_scalar(out=bias[hp][:, :, qs], in0=self.cumL[:, :, h],
                                                             scalar1=self.carry[:, 2 * qs + 1, h:h + 1], scalar2=None, op0=ALU.subtract),
                             reads=['cumL', 'carry'], writes=[('bbias', hp)])

            load_head(0)
            it = 0
            qi = 0
            for h in range(NH):
                hp = h % 2
                hs = slice(h * 128, (h + 1) * 128)
                if h + 1 < NH:
                    load_head(h + 1)
                for qt in range(16):
                    nk = 4 * qt + 4
                    q0 = qt * 512
                    nb, db = 4 + qi % 2, 6 + qi % 2
                    zk = qi % 2
                    qi += 1
                    nbank, dbank = self.ps[nb], self.ps[db]
                    tk.dma('sp', ZT[zk][:], self.ZBT[hs, q0:q0 + 512], writes=[('bz', zk)])

                    def emit_s(kt):
                        sb_ = (it + kt) % 3
                        sbank = self.ps[sb_]
                        j = kt - 4 * qt
                        if j >= 0:
                            self.pe(lambda: nc.tensor.matmul(sbank[:, :], lhsT=self.ident_b, rhs=self.fmask[j], start=True, stop=False),
                                    reads=['c_bf'], writes=[('ps', sb_)], sig=False)
                        self.pe(lambda: nc.tensor.matmul(sbank[:, :], lhsT=KT[hp][:, kt * 128:(kt + 1) * 128], rhs=QT[hp][:, q0:q0 + 512],
                                                         start=(j < 0), stop=True),
                                reads=[('bq', hp), ('bk', hp)], writes=[('ps', sb_)])
                        pk = (it + kt) % 6
                        for hf in range(2):
                            qs = 2 * qt + hf
                            self.act(lambda: nc.scalar.activation(out=PT[pk][:, hf * 256:(hf + 1) * 256], in_=sbank[:, hf * 256:(hf + 1) * 256],
                                                                  func=AF.Exp, scale=SCALE, bias=bias[hp][:, kt, qs:qs + 1]),
                                     reads=[('ps', sb_), ('bbias', hp)], writes=[('bpt', pk)])

                    def emit_pv(kt):
                        pk = (it + kt) % 6
                        self.pe(lambda: nc.tensor.matmul(nbank[:, :], lhsT=V[hp][:, kt, :], rhs=PT[pk][:], start=(kt == 0), stop=(kt == nk - 1)),
                                reads=[('bv', hp), ('bpt', pk)], writes=[('ps', nb)], sig=(kt == nk - 1))
                        self.pe(lambda: nc.tensor.matmul(dbank[:, :], lhsT=self.ones_b, rhs=PT[pk][:], start=(kt == 0), stop=(kt == nk - 1)),
                                reads=['c_bf', ('bpt', pk)], writes=[('ps', db)], sig=True)

                    LA = 2
                    for kt in range(min(LA, nk)):
                        emit_s(kt)
                    for kt in range(nk):
                        if kt + LA < nk:
                            emit_s(kt + LA)
                        emit_pv(kt)
                    it += nk
                    self.dve(lambda: nc.vector.reciprocal(rd[zk][:], dbank[:, :]), reads=[('ps', db)], writes=[('brd', zk)])
                    self.dve(lambda: nc.vector.tensor_tensor(out=on[zk][:], in0=nbank[:, :], in1=rd[zk][:], op=ALU.mult),
                             reads=[('ps', nb), ('brd', zk)], writes=[('bon', zk)])
                    self.pool(lambda: nc.gpsimd.tensor_tensor(out=og[zk][:], in0=on[zk][:], in1=ZT[zk][:], op=ALU.mult),
                              reads=[('bon', zk), ('bz', zk)], writes=[('bog', zk)])
                    tk.dma('pool', self.GBT[hs, q0:q0 + 512], og[zk][:], reads=[('bog', zk)])
            tk.barrier()

    def phase_u(self, l):
        nc, tk = self.nc, self.tk
        GT_v = self.GT.rearrange("(c p) t -> p c t", p=128)
        GAT_v = self.GAT.rearrange("(c p) t -> p c t", p=128)
        GBT_v = self.GBT.rearrange("(c p) t -> p c t", p=128)
        mT_v = self.mT.rearrange("(c p) t -> p c t", p=128)
        TU = 512
        with contextlib.ExitStack() as es:
            sbt = self._sbt(es)
            wst = [sbt("u_wst%d" % i, [128, 1024], F32) for i in range(2)]
            wa = sbt("u_wa", [128, NH, D], BF16)
            wb = sbt("u_wb", [128, NH, D], BF16)
            ga = [sbt("u_ga%d" % i, [128, NH, TU], BF16) for i in range(2)]
            gb = [sbt("u_gb%d" % i, [128, NH, TU], BF16) for i in range(2)]
            gg = [sbt("u_gg%d" % i, [128, 2 * NCH, TU], BF16) for i in range(2)]
            t1 = [sbt("u_t1%d" % i, [128, TU], F32) for i in range(2)]
            t2 = [sbt("u_t2%d" % i, [128, TU], F32) for i in range(2)]
            ms = [sbt("u_ms%d" % i, [128, NCH, TU], BF16) for i in range(2)]
            k = 0
            for (dstw, srcw, nrow) in ((wa, self.w_upa, NH), (wb, self.w_upb, NH)):
                for c in range(2 * nrow):
                    kk = k % 2
                    k += 1
                    hs_ = slice((c % 2) * 1024, (c % 2 + 1) * 1024)
                    tk.dma('sp', wst[kk][:], srcw[l, (c // 2) * 128:(c // 2 + 1) * 128, hs_], writes=[('uwst', kk)])
                    self.pool(lambda: nc.gpsimd.tensor_copy(dstw[:, c // 2, hs_], wst[kk][:]), reads=[('uwst', kk)], writes=['uw'])

            def load_t(tt):
                tp = tt % 2
                ts_ = slice(tt * TU, (tt + 1) * TU)
                tk.dma('sp', ga[tp][:], GAT_v[:, :, ts_], writes=[('uga', tp)])
                tk.dma('sp', gb[tp][:], GBT_v[:, :, ts_], writes=[('ugb', tp)])
                tk.dma('sp', gg[tp][:, 0:NCH, :], GT_v[:, 0:NCH, ts_], writes=[('ugg', tp)])
                tk.dma('sp', gg[tp][:, NCH:2 * NCH, :], GT_v[:, NCH:2 * NCH, ts_], writes=[('ugg', tp)])

            ntt = S // TU
            load_t(0)
            gi = 0
            bi = 0
            for tt in range(ntt):
                tp = tt % 2
                ts_ = slice(tt * TU, (tt + 1) * TU)
                if tt + 1 < ntt:
                    load_t(tt + 1)
                for dc in range(NCH):
                    gk = tp
                    ba, bb = bi % 8, (bi + 1) % 8
                    bi += 2
                    for (bank_i, wt, gsrc, key) in ((ba, wa, ga, 'uga'), (bb, wb, gb, 'ugb')):
                        for wc in range(NH):
                            self.pe(lambda: nc.tensor.matmul(self.ps[bank_i][:, 0:TU], lhsT=wt[:, wc, dc * 128:(dc + 1) * 128], rhs=gsrc[tp][:, wc, :],
                                                             start=(wc == 0), stop=(wc == NH - 1)),
                                    reads=['uw', (key, tp)], writes=[('ps', bank_i)], sig=(wc == NH - 1))
                    k2 = dc % 2
                    self.dve(lambda: nc.vector.tensor_tensor(out=t1[k2][:], in0=self.ps[ba][:, 0:TU], in1=gg[gk][:, dc, :], op=ALU.mult),
                             reads=[('ps', ba), ('ugg', gk)], writes=[('ut1', k2)])
                    self.dve(lambda: nc.vector.tensor_tensor(out=t2[k2][:], in0=self.ps[bb][:, 0:TU], in1=gg[gk][:, NCH + dc, :], op=ALU.mult),
                             reads=[('ps', bb), ('ugg', gk)], writes=[('ut2', k2)])
                    self.pool(lambda: nc.gpsimd.tensor_tensor(out=ms[tp][:, dc, :], in0=t1[k2][:], in1=t2[k2][:], op=ALU.add),
                              reads=[('ut1', k2), ('ut2', k2)], writes=[('ums', tp)])
                tk.dma('pool', mT_v[:, :, ts_], ms[tp][:], reads=[('ums', tp)])
            tk.barrier()
        with contextlib.ExitStack() as es:
            sbt = self._sbt(es)
            wst = [sbt("u2_wst%d" % i, [128, 1024], F32) for i in range(2)]
            wo = sbt("u2_wo", [128, NCH, D], BF16)
            mt = [sbt("u2_m%d" % i, [128, NCH, 512], BF16) for i in range(2)]
            ys = [sbt("u2_ys%d" % i, [128, 512], F32) for i in range(3)]
            for c in range(2 * NCH):
                kk = c % 2
                hs_ = slice((c % 2) * 1024, (c % 2 + 1) * 1024)
                tk.dma('sp', wst[kk][:], self.w_out[l, (c // 2) * 128:(c // 2 + 1) * 128, hs_], writes=[('uwst', kk)])
                self.pool(lambda: nc.gpsimd.tensor_copy(wo[:, c // 2, hs_], wst[kk][:]), reads=[('uwst', kk)], writes=['uw'])

            def load_m(tt):
                tk.dma('sp', mt[tt % 2][:], mT_v[:, :, tt * 512:(tt + 1) * 512], writes=[('um', tt % 2)])

            load_m(0)
            bi = 0
            yi = 0
            for tt in range(16):
                tp = tt % 2
                ts_ = slice(tt * 512, (tt + 1) * 512)
                if tt + 1 < 16:
                    load_m(tt + 1)
                for ec in range(NCH):
                    bk = bi % 4
                    bi += 1
                    for dc in range(NCH):
                        self.pe(lambda: nc.tensor.matmul(self.ps[bk][:, :], lhsT=wo[:, dc, ec * 128:(ec + 1) * 128], rhs=mt[tp][:, dc, :],
                                                         start=(dc == 0), stop=(dc == NCH - 1)),
                                reads=['uw', ('um', tp)], writes=[('ps', bk)], sig=(dc == NCH - 1))
                    yk = yi % 3
                    yi += 1
                    self.act(lambda: nc.scalar.copy(out=ys[yk][:], in_=self.ps[bk][:, :]), reads=[('ps', bk)], writes=[('uys', yk)])
                    tk.dma('pool', self.yT[tt, ec * 128:(ec + 1) * 128, :], ys[yk][:], reads=[('uys', yk)], writes=[('yT', tt)])
                tk.cc(lambda: nc.gpsimd.collective_compute("AllReduce", ALU.add, replica_groups=self.groups,
                                                           ins=[self.yT[tt].opt()], outs=[self.yTs[tt].opt()]),
                      reads=[('yT', tt)], writes=[('yTs', tt)])
            tk.barrier(keep='yTs', wait_cc=False)

    def phase_ln(self, l, final):
        nc, tk = self.nc, self.tk
        TL = 256
        NG = 4
        xTf_v = self.xTf.rearrange("(c p) t -> p c t", p=128)
        xTb_v = self.xTb.rearrange("(c p) t -> p c t", p=128)
        y_v = self.yTs.rearrange("n (c p) t -> n p c t", p=128)
        with contextlib.ExitStack() as es:
            sbt = self._sbt(es)
            xs = [sbt("l_x%d" % i, [128, NCH, TL], F32) for i in range(2)]
            yy = [sbt("l_y%d" % i, [128, NCH, TL], F32) for i in range(2)]
            sq = [sbt("l_sq%d" % i, [128, NCH, TL], F32) for i in range(2)]
            xb = [sbt("l_xb%d" % i, [128, NCH, TL], BF16) for i in range(2)]
            st = [sbt("l_st%d" % i, [128, 4, TL], F32) for i in range(2)]
            ot = [sbt("l_ot%d" % i, [128, D], F32) for i in range(2)]
            g = sbt("l_g", [128, NCH], F32)
            b = sbt("l_b", [128, NCH], F32)
            tk.dma('sp', g[:], self.lng[l], writes=['lg'])
            tk.dma('sp', b[:], self.lnb[l], writes=['lb'])
            ntile = S // TL
            oi = [0]

            def stage_a(i):
                p_ = i % 2
                ts_ = slice(i * TL, (i + 1) * TL)
                X, Y = xs[p_], yy[p_]
                gk = [('lx', p_, q_) for q_ in range(NG)]
                tk.dma('sp', X[:], xTf_v[:, :, ts_], writes=gk)
                tk.dma('sp', Y[:], y_v[(i * TL) // 512][:, :, (i * TL) % 512:(i * TL) % 512 + TL],
                       reads=[('yTs', (i * TL) // 512)], writes=[('ly', p_)])
                Xf = X[:].rearrange("p c t -> p (c t)")
                Yf = Y[:].rearrange("p c t -> p (c t)")
                self.dve(lambda: nc.vector.scalar_tensor_tensor(out=Xf, in0=Xf, scalar=ALPHA, in1=Yf, op0=ALU.mult, op1=ALU.add),
                         reads=[('ly', p_)], writes=gk)
                self.act(lambda: nc.scalar.activation(out=sq[p_][:].rearrange("p c t -> p (c t)"), in_=Xf, func=AF.Square),
                         reads=gk, writes=[('lsq', p_)])
                b1, b2 = p_, 2 + p_
                for c in range(NCH):
                    self.pe(lambda: nc.tensor.matmul(self.ps[b1][:, 0:TL], lhsT=self.ones, rhs=X[:, c, :], start=(c == 0), stop=(c == NCH - 1)),
                            reads=['c_f32'] + gk, writes=[('ps', b1)], sig=(c == NCH - 1))
                for c in range(NCH):
                    self.pe(lambda: nc.tensor.matmul(self.ps[b2][:, 0:TL], lhsT=self.ones, rhs=sq[p_][:, c, :], start=(c == 0), stop=(c == NCH - 1)),
                            reads=['c_f32', ('lsq', p_)], writes=[('ps', b2)], sig=(c == NCH - 1))

            def stage_b(i):
                p_ = i % 2
                ts_ = slice(i * TL, (i + 1) * TL)
                X, ST = xs[p_], st[p_]
                b1, b2 = p_, 2 + p_
                sk = ('lst', p_)
                mean, msq, var, rstd = ST[:, 0, :], ST[:, 1, :], ST[:, 2, :], ST[:, 3, :]
                self.dve(lambda: nc.vector.tensor_scalar(out=mean, in0=self.ps[b1][:, 0:TL], scalar1=1.0 / D, scalar2=None, op0=ALU.mult),
                         reads=[('ps', b1)], writes=[sk])
                self.dve(lambda: nc.vector.tensor_tensor(out=msq, in0=mean, in1=mean, op=ALU.mult), reads=[sk], writes=[sk])
                self.dve(lambda: nc.vector.scalar_tensor_tensor(out=var, in0=self.ps[b2][:, 0:TL], scalar=1.0 / D, in1=msq, op0=ALU.mult, op1=ALU.subtract),
                         reads=[('ps', b2), sk], writes=[sk])
                self.dve(lambda: nc.vector.tensor_scalar(out=var, in0=var, scalar1=LN_EPS, scalar2=None, op0=ALU.add), reads=[sk], writes=[sk])
                self.act(lambda: nc.scalar.activation(out=var, in_=var, func=AF.Sqrt), reads=[sk], writes=[sk])
                self.dve(lambda: nc.vector.reciprocal(rstd, var), reads=[sk], writes=[sk])
                cg = NCH // NG
                mean_b = mean.unsqueeze(1).to_broadcast([128, cg, TL])
                rstd_b = rstd.unsqueeze(1).to_broadcast([128, cg, TL])
                for q_ in range(NG):
                    k_ = ('lx', p_, q_)
                    Xg = X[:, q_ * cg:(q_ + 1) * cg, :]
                    self.dve(lambda: nc.vector.tensor_tensor(out=Xg, in0=Xg, in1=mean_b, op=ALU.subtract), reads=[sk], writes=[k_])
                    self.pool(lambda: nc.gpsimd.tensor_tensor(out=Xg, in0=Xg, in1=rstd_b, op=ALU.mult), reads=[sk], writes=[k_])
                    for c in range(q_ * cg, (q_ + 1) * cg):
                        self.act(lambda: nc.scalar.activation(out=X[:, c, :], in_=X[:, c, :], func=AF.Identity, scale=g[:, c:c + 1], bias=b[:, c:c + 1]),
                                 reads=['lg', 'lb'], writes=[k_])
                    if not final:
                        self.pool(lambda: nc.gpsimd.tensor_copy(xb[p_][:, q_ * cg:(q_ + 1) * cg, :], Xg), reads=[k_], writes=[('lxb', p_)])
                gk = [('lx', p_, q_) for q_ in range(NG)]
                if not final:
                    tk.dma('act', xTf_v[:, :, ts_], X[:], reads=gk)
                    tk.dma('sp', xTb_v[:, :, ts_], xb[p_][:], reads=[('lxb', p_)])
                else:
                    for sub in range(TL // 128):
                        ok = oi[0] % 2
                        oi[0] += 1
                        for cgi in range(4):
                            bk = 4 + (oi[0] * 4 + cgi) % 4
                            bank = self.ps[bk]
                            for c4 in range(4):
                                c = cgi * 4 + c4
                                self.pe(lambda: nc.tensor.transpose(out=bank[:, c4 * 128:(c4 + 1) * 128], in_=X[:, c, sub * 128:(sub + 1) * 128], identity=self.ident),
                                        reads=gk + ['c_f32'], writes=[('ps', bk)], sig=(c4 == 3))
                            self.act(lambda: nc.scalar.copy(out=ot[ok][:, cgi * 512:(cgi + 1) * 512], in_=bank[:, :]),
                                     reads=[('ps', bk)], writes=[('lot', ok)])
                        r0 = i * TL + sub * 128
                        tk.dma('act', self.out[r0:r0 + 128, :], ot[ok][:], reads=[('lot', ok)])

            stage_a(0)
            for i in range(ntile):
                if i + 1 < ntile:
                    stage_a(i + 1)
                stage_b(i)
            tk.barrier()

    def build(self, phases=None):
        self.load_consts()
        self.phase_x0()
        for l in range(self.n_layers):
            self.phase_p(l)
            self.phase_a()
            self.phase_b()
            self.phase_u(l)
            self.phase_ln(l, final=(l == self.n_layers - 1))
        self.tk.barrier()
        return self.nc


def _consts():
    p = np.arange(128)
    ident = np.eye(128, dtype=np.float32)
    ones = np.ones((128, 128), np.float32)
    tri = (p[:, None] <= p[None, :]).astype(np.float32)
    perm = np.zeros((128, 128), np.float32)
    perm[(p + 64) % 128, p] = 1.0
    cf32 = np.concatenate([ident, ones, tri, perm], axis=1)
    q = np.arange(512)
    fm = [np.where(q[None, :] >= 128 * j + p[:, None], 0.0, NEG).astype(np.float32) for j in range(4)]
    q1 = np.arange(128)
    m_prev = np.where(q1[None, :] <= p[:, None], 0.0, NEG).astype(np.float32)
    m_diag = np.where(p[:, None] <= q1[None, :], 0.0, NEG).astype(np.float32)
    cbf = np.concatenate([ident, ones] + fm + [m_prev, m_diag], axis=1).astype(ml_dtypes.bfloat16)
    half = HD // 2
    inv_freq = (np.float32(10000.0) ** (-np.arange(half, dtype=np.float32) / np.float32(half))).astype(np.float32)
    ang = np.arange(S, dtype=np.float32)[None, :] * inv_freq[:, None]
    cos = np.cos(ang).astype(np.float32)
    sin = np.sin(ang).astype(np.float32)
    cosT = np.concatenate([cos, cos], axis=0)
    sinT = np.concatenate([-sin, sin], axis=0)
    return cf32, cbf, np.ascontiguousarray(cosT), np.ascontiguousarray(sinT)


def _col_index(p):
    blk = NHA * HD
    hsel = np.arange(p * W, (p + 1) * W)
    idx = [b * blk + hsel for b in (0, 1, 3, 4, 5, 7)]
    g0 = 8 * blk + NHA
    idx.append(np.arange(g0, g0 + 2 * D))
    idx.append(2 * blk + hsel)
    idx.append(6 * blk + hsel)
    idx.append(8 * blk + np.arange(p * NH, (p + 1) * NH))
    return np.concatenate(idx)


def make_in_maps(inputs, n_layers=DEPTH, cores=8):
    x = np.asarray(inputs["x"], np.float32)
    cf32, cbf, cosT, sinT = _consts()
    common = dict(
        w_out=np.ascontiguousarray(np.asarray(inputs["w_out"], np.float32)[:n_layers]),
        bgt=np.ascontiguousarray(np.asarray(inputs["b_gate"], np.float32)[:n_layers].reshape(n_layers, 32, 128).transpose(0, 2, 1)),
        lng=np.ascontiguousarray(np.asarray(inputs["ln_g"], np.float32)[:n_layers].reshape(n_layers, NCH, 128).transpose(0, 2, 1)),
        lnb=np.ascontiguousarray(np.asarray(inputs["ln_b"], np.float32)[:n_layers].reshape(n_layers, NCH, 128).transpose(0, 2, 1)),
        cosT=cosT, sinT=sinT, cf32=cf32, cbf=cbf)
    half = {}
    for p in (0, 1):
        hsel = slice(p * W, (p + 1) * W)
        half[p] = dict(
            w_sel=np.ascontiguousarray(np.asarray(inputs["w_in"], np.float32)[:n_layers][:, :, _col_index(p)]),
            w_upa=np.ascontiguousarray(np.asarray(inputs["w_up_a"], np.float32)[:n_layers, hsel, :]),
            w_upb=np.ascontiguousarray(np.asarray(inputs["w_up_b"], np.float32)[:n_layers, hsel, :]),
            bfr=np.ascontiguousarray(np.broadcast_to(
                np.asarray(inputs["b_forget"], np.float32)[:n_layers, None, p * NH:(p + 1) * NH], (n_layers, 128, NH))))
    maps = []
    for core in range(cores):
        b, p = core // 2, core % 2
        m = dict(common)
        m.update(half[p])
        m["x"] = np.ascontiguousarray(x[b])
        maps.append(m)
    return maps


def kernel(x, w_in, b_forget, b_gate, w_up_a, w_up_b, w_out, ln_g, ln_b):
    inputs = dict(x=x, w_in=w_in, b_forget=b_forget, b_gate=b_gate, w_up_a=w_up_a, w_up_b=w_up_b,
                  w_out=w_out, ln_g=ln_g, ln_b=ln_b)
    bld = Builder(n_layers=DEPTH)
    nc = bld.build()
    in_maps = make_in_maps(inputs)
    res = run_bass_kernel_spmd(nc, in_maps, core_ids=list(range(8)))
    return np.stack([res.results[2 * b]["out"] for b in range(4)], axis=0).astype(np.float32)
```
